# Optimizing a Trainium2 kernel written in Bass

```python
import math
import jax, jax.numpy as jnp
from jax import lax
import numpy as np

D_MODEL = 1024
BATCH = 8
SEQ = 2048
DEPTH = 1
DEC_BATCH = 32
DEC_SEQ = 64
PAST_LEN = 1024

CHUNK = 64
D_MIX = D_MODEL
HEAD_DIM = 64
D_ATTN = D_MIX // 2
D_CMLP = D_MIX - D_ATTN
N_HEADS = D_ATTN // HEAD_DIM
N_KV_HEADS = 2
GQA_GROUP = N_HEADS // N_KV_HEADS
D_KV = N_KV_HEADS * HEAD_DIM
WINDOW = 128
N_PREV_CHUNKS = WINDOW // CHUNK
ROPE_DIM = HEAD_DIM // 4
ROPE_THETA = 500000.0
CMLP_BLOCK = 128
CMLP_GROUP_DIM = 64
CMLP_GROUPS = D_CMLP // CMLP_GROUP_DIM
D_IN = D_ATTN + 2 * D_KV + 2 * D_CMLP
D_FF = ((-(-8 * D_MODEL // 3) + 255) // 256) * 256
ALPHA = (2 * DEPTH) ** 0.25
BETA = (8 * DEPTH) ** -0.25
LN_EPS = 1e-5
NEG_INF = -1e30

kernel_name = "hybrid_swa_sink_chunk_gmlp_stream_step"


def _layernorm(x, g, b):
    xf = x.astype(jnp.float32)
    mu = jnp.mean(xf, -1, keepdims=True)
    var = jnp.mean(jnp.square(xf - mu), -1, keepdims=True)
    return ((xf - mu) * lax.rsqrt(var + LN_EPS) * g.astype(jnp.float32) + b.astype(jnp.float32)).astype(x.dtype)


def _rmsnorm(x, g):
    xf = x.astype(jnp.float32)
    ms = jnp.mean(jnp.square(xf), -1, keepdims=True)
    return (xf * lax.rsqrt(ms + LN_EPS) * g.astype(jnp.float32)).astype(x.dtype)


def _rope(x, pos):
    half = ROPE_DIM // 2
    inv = jnp.power(ROPE_THETA, -jnp.arange(half, dtype=jnp.float32) * (2.0 / ROPE_DIM))
    ang = pos.astype(jnp.float32)[:, None] * inv[None, :]
    cos = jnp.cos(ang)[:, None, :]
    sin = jnp.sin(ang)[:, None, :]
    xr = x[..., :ROPE_DIM].astype(jnp.float32)
    x1, x2 = xr[..., :half], xr[..., half:]
    rot = jnp.concatenate([x1 * cos - x2 * sin, x2 * cos + x1 * sin], -1)
    return jnp.concatenate([rot.astype(x.dtype), x[..., ROPE_DIM:]], -1)


def _sink_attention(q, k, v, valid, sinks):
    s = jnp.einsum('...qkgd,...jkd->...kgqj', q.astype(jnp.float32), k.astype(jnp.float32)) * (HEAD_DIM ** -0.5)
    if valid is not None:
        s = jnp.where(valid, s, NEG_INF)
    sink = sinks.astype(jnp.float32).reshape(N_KV_HEADS, GQA_GROUP)[..., None, None]
    m = jnp.maximum(jnp.max(s, -1, keepdims=True), sink)
    p = jnp.exp(s - m)
    denom = jnp.sum(p, -1, keepdims=True) + jnp.exp(sink - m)
    o = jnp.einsum('...kgqj,...jkd->...qkgd', p / denom, v.astype(jnp.float32))
    return o.astype(q.dtype)


def _project(x, w_in, ln_v_g, ln_v_b):
    B, L = x.shape[:2]
    h = x @ w_in
    q = h[..., :D_ATTN].reshape(B, L, N_HEADS, HEAD_DIM)
    k = h[..., D_ATTN:D_ATTN + D_KV].reshape(B, L, N_KV_HEADS, HEAD_DIM)
    v = h[..., D_ATTN + D_KV:D_ATTN + 2 * D_KV].reshape(B, L, N_KV_HEADS, HEAD_DIM)
    uv = jax.nn.gelu(h[..., D_ATTN + 2 * D_KV:])
    u = uv[..., :D_CMLP].reshape(B, L, CMLP_GROUPS, CMLP_GROUP_DIM)
    vm = _layernorm(uv[..., D_CMLP:], ln_v_g, ln_v_b).reshape(B, L, CMLP_GROUPS, CMLP_GROUP_DIM)
    return q, k, v, u, vm


def _window_attn_prompt(q, k, v, sinks):
    B, S = q.shape[:2]
    n_c = S // CHUNK
    pad = N_PREV_CHUNKS * CHUNK
    pos = jnp.arange(S)
    q = _rope(q, pos)
    k = _rope(k, pos)
    kp = jnp.pad(k, ((0, 0), (pad, 0), (0, 0), (0, 0))).reshape(B, n_c + N_PREV_CHUNKS, CHUNK, N_KV_HEADS, HEAD_DIM)
    vp = jnp.pad(v, ((0, 0), (pad, 0), (0, 0), (0, 0))).reshape(B, n_c + N_PREV_CHUNKS, CHUNK, N_KV_HEADS, HEAD_DIM)
    kb = jnp.concatenate([kp[:, i:i + n_c] for i in range(N_PREV_CHUNKS + 1)], axis=2)
    vb = jnp.concatenate([vp[:, i:i + n_c] for i in range(N_PREV_CHUNKS + 1)], axis=2)
    qb = q.reshape(B, n_c, CHUNK, N_KV_HEADS, GQA_GROUP, HEAD_DIM)
    key_pos = jnp.arange(n_c)[:, None] * CHUNK - pad + jnp.arange((N_PREV_CHUNKS + 1) * CHUNK)[None, :]
    valid = (key_pos >= 0)[:, None, None, None, :]
    o = _sink_attention(qb, kb, vb, valid, sinks)
    return o.reshape(B, S, D_ATTN), k, v


def _window_attn_sample(q, k, v, cache_k, cache_v, sinks):
    Bd, L = q.shape[:2]
    pos = PAST_LEN + jnp.arange(L)
    q = _rope(q, pos)
    k = _rope(k, pos)
    kk = jnp.concatenate([cache_k.astype(k.dtype), k], axis=1)
    vv = jnp.concatenate([cache_v.astype(v.dtype), v], axis=1)
    qg = q.reshape(Bd, L, N_KV_HEADS, GQA_GROUP, HEAD_DIM)
    o = _sink_attention(qg, kk, vv, None, sinks)
    return o.reshape(Bd, L, D_ATTN), k, v


def _spatial_gate(u, vm, w_s, b_s):
    lb = u.shape[2]
    i = jnp.arange(lb)
    mask = (i[None, :] // CHUNK) <= (i[:, None] // CHUNK)
    w = jnp.where(mask[None], w_s[:, :lb, :lb], jnp.zeros((), w_s.dtype))
    s = jnp.einsum('gij,bnjgd->bnigd', w, vm) + jnp.transpose(b_s[:, :lb])[:, :, None]
    return u * s


def _merge(attn_o, cmlp_o, norm_attn_g, norm_cmlp_g, w_out):
    cat = jnp.concatenate([_rmsnorm(attn_o, norm_attn_g), _rmsnorm(cmlp_o, norm_cmlp_g)], -1)
    return cat @ w_out


def _post(x, mix, ln1_g, ln1_b, w_gate_up, w_down, ln2_g, ln2_b):
    h = _layernorm(ALPHA * x + mix, ln1_g, ln1_b)
    gu = h @ w_gate_up
    f = (jax.nn.silu(gu[..., :D_FF]) * gu[..., D_FF:]) @ w_down
    return _layernorm(ALPHA * h + f, ln2_g, ln2_b)


def setup_inputs(seed: int = 0) -> dict:
    key = jax.random.key(seed)
    ks = jax.random.split(key, 20)
    cw = min(WINDOW, PAST_LEN)
    f32 = jnp.float32
    nrm = lambda k, shape, scale: (jax.random.normal(k, shape, f32) * scale)
    return {
        "x_prompt": nrm(ks[0], (BATCH, SEQ, D_MODEL), 1.0),
        "x_sample": nrm(ks[1], (DEC_BATCH, DEC_SEQ, D_MODEL), 1.0),
        "cache_win_k": nrm(ks[2], (DEPTH, DEC_BATCH, cw, N_KV_HEADS, HEAD_DIM), 1.0),
        "cache_win_v": nrm(ks[3], (DEPTH, DEC_BATCH, cw, N_KV_HEADS, HEAD_DIM), 1.0),
        "w_in": nrm(ks[4], (DEPTH, D_MODEL, D_IN), D_MODEL ** -0.5),
        "ln_v_g": 1.0 + nrm(ks[5], (DEPTH, D_CMLP), 0.02),
        "ln_v_b": nrm(ks[6], (DEPTH, D_CMLP), 0.02),
        "attn_sinks": nrm(ks[7], (DEPTH, N_HEADS), 1.0),
        "w_spatial": nrm(ks[8], (DEPTH, CMLP_GROUPS, CMLP_BLOCK, CMLP_BLOCK), CMLP_BLOCK ** -0.5),
        "b_spatial": 1.0 + nrm(ks[9], (DEPTH, CMLP_GROUPS, CMLP_BLOCK), 0.1),
        "norm_attn_g": 1.0 + nrm(ks[10], (DEPTH, D_ATTN), 0.02),
        "norm_cmlp_g": 1.0 + nrm(ks[11], (DEPTH, D_CMLP), 0.02),
        "w_out": nrm(ks[12], (DEPTH, D_MIX, D_MODEL), BETA * D_MIX ** -0.5),
        "ln1_g": 1.0 + nrm(ks[13], (DEPTH, D_MODEL), 0.02),
        "ln1_b": nrm(ks[14], (DEPTH, D_MODEL), 0.02),
        "w_gate_up": nrm(ks[15], (DEPTH, D_MODEL, 2 * D_FF), D_MODEL ** -0.5),
        "w_down": nrm(ks[16], (DEPTH, D_FF, D_MODEL), BETA * D_FF ** -0.5),
        "ln2_g": 1.0 + nrm(ks[17], (DEPTH, D_MODEL), 0.02),
        "ln2_b": nrm(ks[18], (DEPTH, D_MODEL), 0.02),
    }


def reference(x_prompt, x_sample, cache_win_k, cache_win_v, w_in, ln_v_g, ln_v_b, attn_sinks, w_spatial, b_spatial,
              norm_attn_g, norm_cmlp_g, w_out, ln1_g, ln1_b, w_gate_up, w_down, ln2_g, ln2_b):
    yp, ys = x_prompt, x_sample
    B, S = yp.shape[:2]
    Bd, L = ys.shape[:2]
    cw_p = min(WINDOW, S)
    kp_rows, vp_rows, ks_rows, vs_rows, ms_rows = [], [], [], [], []
    for l in range(DEPTH):
        q, k, v, u, vm = _project(yp, w_in[l], ln_v_g[l], ln_v_b[l])
        ao, kr, vr = _window_attn_prompt(q, k, v, attn_sinks[l])
        nb = S // CMLP_BLOCK
        co = _spatial_gate(u.reshape(B, nb, CMLP_BLOCK, CMLP_GROUPS, CMLP_GROUP_DIM),
                           vm.reshape(B, nb, CMLP_BLOCK, CMLP_GROUPS, CMLP_GROUP_DIM),
                           w_spatial[l], b_spatial[l]).reshape(B, S, D_CMLP)
        mix = _merge(ao, co, norm_attn_g[l], norm_cmlp_g[l], w_out[l])
        yp = _post(yp, mix, ln1_g[l], ln1_b[l], w_gate_up[l], w_down[l], ln2_g[l], ln2_b[l])
        kp_rows.append(kr[:, S - cw_p:])
        vp_rows.append(vr[:, S - cw_p:])
        q, k, v, u, vm = _project(ys, w_in[l], ln_v_g[l], ln_v_b[l])
        ao, kr, vr = _window_attn_sample(q, k, v, cache_win_k[l], cache_win_v[l], attn_sinks[l])
        co = _spatial_gate(u[:, None], vm[:, None], w_spatial[l], b_spatial[l]).reshape(Bd, L, D_CMLP)
        mix = _merge(ao, co, norm_attn_g[l], norm_cmlp_g[l], w_out[l])
        ys = _post(ys, mix, ln1_g[l], ln1_b[l], w_gate_up[l], w_down[l], ln2_g[l], ln2_b[l])
        ks_rows.append(kr)
        vs_rows.append(vr)
        ms_rows.append(vm)
    new_win_k_prompt = jnp.stack(kp_rows)
    new_win_v_prompt = jnp.stack(vp_rows)
    new_k_sample = jnp.stack(ks_rows)
    new_v_sample = jnp.stack(vs_rows)
    new_cmlp_v_sample = jnp.stack(ms_rows)
    return (yp, ys, new_win_k_prompt, new_win_v_prompt, new_k_sample, new_v_sample, new_cmlp_v_sample)
```

```python
import numpy as np
from contextlib import ExitStack
import concourse.bass as bass
import concourse.mybir as mybir
from concourse.bass_utils import run_bass_kernel_spmd

F32 = mybir.dt.float32
BF16 = mybir.dt.bfloat16
AF = mybir.ActivationFunctionType
ALU = mybir.AluOpType

N_CORES = 8
D = 1024
D_IN = 1792
D_FF = 2816
NT = 18
TPS = 9
NSG = 2
GC = 2
NG = D_FF // (128 * GC)
NSLOT = 3
TG = 384
NTG = TPS * 128 // TG
ALPHA = 2.0 ** 0.25
EPS = 1e-5
TILES = [("p", j) for j in range(8)] + [("s", 0)] + [("p", j) for j in range(8, 16)] + [("s", 1)]


class Op:
    __slots__ = ("eng", "fn", "reads", "writes", "dma", "lane", "idx", "sig", "deps", "tag")

    def __init__(self, eng, fn, reads, writes, dma, lane):
        self.eng, self.fn = eng, fn
        self.reads, self.writes = tuple(reads), tuple(writes)
        self.dma, self.lane = dma, lane
        self.sig = None
        self.deps = []
        self.tag = None


class Sched:
    ENGS = ("pe", "act", "dve", "pool", "sp")

    def __init__(self, nc):
        self.nc = nc
        self.ops = []

    tag = ""

    def add(self, eng, fn, reads=(), writes=()):
        op = Op(eng, fn, reads, writes, False, None)
        op.tag = self.tag + " w=" + ",".join(map(str, writes))
        self.ops.append(op)
        return op

    def dma(self, queue, out, in_, reads=(), writes=(), lane=None, **kw):
        fn = lambda e, out=out, in_=in_, kw=kw: e.dma_start(out=out, in_=in_, **kw)
        op = Op(queue, fn, reads, writes, True, lane)
        op.tag = self.tag + " dma " + str(lane)
        self.ops.append(op)
        return op

    def _analyze(self):
        last_w, readers = {}, {}
        for i, op in enumerate(self.ops):
            op.idx = i
            deps = {}
            for k in op.reads:
                w = last_w.get(k)
                if w is not None:
                    deps[w.idx] = (w, True)
            for k in op.writes:
                w = last_w.get(k)
                if w is not None and w.idx not in deps:
                    deps[w.idx] = (w, True)
                rd = readers.get(k)
                if rd is not None:
                    for r in list(rd[0].values()) + rd[1]:
                        if r.idx not in deps and r is not op:
                            deps[r.idx] = (r, False)
            need = []
            for (p, raw) in deps.values():
                if p.dma or p.eng != op.eng:
                    need.append(p)
                elif raw and (p.eng != "pe" or op.dma):
                    need.append(p)
            op.deps = need
            for k in op.reads:
                rd = readers.setdefault(k, ({}, []))
                if op.dma:
                    rd[1].append(op)
                else:
                    rd[0][op.eng] = op
            for k in op.writes:
                last_w[k] = op
                readers[k] = ({}, [])
        for op in self.ops:
            for p in op.deps:
                if p.sig is None:
                    p.sig = -1
        cnt = {e: 0 for e in self.ENGS}
        lane_cnt = {}
        for op in self.ops:
            if op.dma:
                lane_cnt[op.lane] = lane_cnt.get(op.lane, 0) + 1
                op.sig = 16 * lane_cnt[op.lane]
            elif op.sig == -1:
                cnt[op.eng] += 1
                op.sig = cnt[op.eng]
        self.lane_final = {l: 16 * c for l, c in lane_cnt.items()}

    def emit(self, stack, final_wait_lanes=()):
        nc = self.nc
        self._analyze()
        sems = {}
        for e in self.ENGS:
            sems[("eng", e)] = stack.enter_context(nc.semaphore("s_" + e))
        for l in self.lane_final:
            sems[("lane", l)] = stack.enter_context(nc.semaphore("l_" + str(l)))
        block = stack.enter_context(nc.Block())
        per_eng = {e: [op for op in self.ops if op.eng == e] for e in self.ENGS}

        def body(ename, engine):
            waited = {}
            for op in per_eng[ename]:
                for p in op.deps:
                    key = ("lane", p.lane) if p.dma else ("eng", p.eng)
                    if waited.get(key, 0) < p.sig:
                        engine.wait_ge(sems[key], p.sig)
                        waited[key] = p.sig
                inst = op.fn(engine)
                if op.dma:
                    inst.then_inc(sems[("lane", op.lane)], 16)
                elif op.sig is not None:
                    inst.then_inc(sems[("eng", ename)], 1)
            if ename == "sp":
                for l in final_wait_lanes:
                    if l in self.lane_final:
                        engine.wait_ge(sems[("lane", l)], self.lane_final[l])

        block.tensor(lambda e: body("pe", e))
        block.scalar(lambda e: body("act", e))
        block.vector(lambda e: body("dve", e))
        block.gpsimd(lambda e: body("pool", e))
        block.sync(lambda e: body("sp", e))


def build_program():
    nc = bass.Bass("TRN2", target_bir_lowering=False)

    def din(name, shape):
        return nc.dram_tensor(name, list(shape), F32, kind="ExternalInput").ap()

    def dout(name, shape):
        return nc.dram_tensor(name, list(shape), F32, kind="ExternalOutput").ap()

    x_d = din("x", [NT * 128, D])
    ck_d = din("cache_k", [4, 128, 128])
    cv_d = din("cache_v", [4, 128, 128])
    w_in_d = din("w_in", [D, D_IN])
    w_out_d = din("w_out", [D, D])
    w_gu_d = din("w_gu", [D, 2 * D_FF])
    w_dn_d = din("w_dn", [D_FF, D])
    lnv_g_d = din("ln_v_g", [1, 512])
    lnv_b_d = din("ln_v_b", [1, 512])
    sinks_d = din("sinks", [1, 8])
    wsp_d = din("w_sp", [8, 128, 128])
    bsp_d = din("b_sp", [8, 128])
    ng_a_d = din("ng_a", [1, 512])
    ng_c_d = din("ng_c", [1, 512])
    ln1_g_d = din("ln1_g", [1, D])
    ln1_b_d = din("ln1_b", [1, D])
    ln2_g_d = din("ln2_g", [1, D])
    ln2_b_d = din("ln2_b", [1, D])
    ident_d = din("c_ident", [128, 128])
    rope_d = din("c_rope", [128, 17 * 16])
    mask_d = din("c_mask", [128, 128])

    y_d = dout("y", [NT * 128, D])
    kp_d = dout("kp", [128, 128])
    vp_d = dout("vp", [128, 128])
    ks_d = dout("ks", [256, 128])
    vs_d = dout("vs", [256, 128])
    ms_d = dout("ms", [256, 512])

    st = ExitStack()
    with st:
        def sb(name, shape, dt=F32):
            return st.enter_context(nc.sbuf_tensor(name, list(shape), dt))

        def psb(name, shape, dt=F32):
            return st.enter_context(nc.psum_tensor(name, list(shape), dt))

        w_in = sb("w_in_sb", [128, 8, D_IN], BF16)
        w_out = sb("w_out_sb", [128, 8, D], BF16)
        ring_g = [sb(f"ring_g{i}", [128, 8, GC * 128], BF16) for i in range(NSLOT)]
        ring_u = [sb(f"ring_u{i}", [128, 8, GC * 128], BF16) for i in range(NSLOT)]
        ring_d = [sb(f"ring_d{i}", [128, GC, D], BF16) for i in range(NSLOT)]
        acc = sb("acc", [128, TPS, D], F32)
        hT = sb("hT", [128, 8, TPS * 128], BF16)
        ident = sb("ident", [128, 128], BF16)
        rope = sb("rope", [128, 17, 16], F32)
        maskc = sb("maskc", [128, 128], F32)
        WT = sb("WT", [128, 8, 128], BF16)
        WTs = sb("WTs", [128, 8, 128], BF16)
        bT = sb("bT", [128, 8], F32)
        bTs = sb("bTs", [128, 8], F32)
        gv = sb("gv", [128, 512], F32)
        bv = sb("bv", [128, 512], F32)
        g1 = sb("g1", [128, D], F32)
        b1 = sb("b1", [128, D], F32)
        gcat = sb("gcat", [128, 8], F32)
        esink = sb("esink", [128, 8], F32)
        c_eps = sb("c_eps", [128, 1], F32)
        c_mh = sb("c_mh", [128, 1], F32)
        dmy = sb("dmy", [128, 1], F32)
        c_eps_r = sb("c_eps_r", [128, 1], F32)
        c_eps_l = sb("c_eps_l", [128, 1], F32)
        kTc = sb("kTc", [128, 4, 128], BF16)
        vaug_c = sb("vaug_c", [128, 4, 2, 65], BF16)

        xf = [sb(f"xf{i}", [128, D], F32) for i in range(2)]
        xb1 = sb("xb", [128, D], BF16)
        xT = sb("xT", [128, 8, 128], BF16)
        xr = sb("xr", [128, 10, 16], F32)
        qkb = sb("qkb", [128, 640], BF16)
        rt = [sb(f"rt{i}", [128, 10, 8], F32) for i in range(4)]
        kout = sb("kout", [128, 128], F32)
        vf = sb("vf", [128, 128], F32)
        vaug = [sb(f"vaug{i}", [128, 2, 65], BF16) for i in range(3)]
        kT = [sb(f"kT{i}", [128, 128], BF16) for i in range(3)]
        qT = sb("qT", [128, 4, 128], BF16)
        ug = sb("ug", [128, 512], F32)
        vg = sb("vg", [128, 512], F32)
        vm = sb("vm", [128, 512], F32)
        vmb = sb("vmb", [128, 512], BF16)
        PT = [[sb(f"PT{g}{k}", [128, 4, 128], BF16) for k in range(3)] for g in range(2)]
        den = sb("den", [128, 8], F32)
        rden = sb("rden", [128, 8], F32)
        co = sb("co", [128, 512], F32)
        cat = sb("cat", [128, D], BF16)
        catT = sb("catT", [128, 8, 128], BF16)
        pre = sb("pre", [128, D], F32)
        hf = sb("hf", [128, D], F32)
        hb = sb("hb", [128, D], BF16)
        stt_v = sb("stt_v", [128, 6], F32)
        mv_v = sb("mv_v", [128, 2], F32)
        stt = sb("stt", [128, 12], F32)
        mv = sb("mv", [128, 2], F32)
        sm = {n: sb("sm_" + n, [128, 1], F32) for n in ("v", "a1", "a2", "c1", "c2", "l1", "l2", "l1b", "l2b", "m0", "m1", "e0", "e1", "q0", "q1", "r0", "r1")}
        sttb = sb("sttb", [128, 12], F32)
        mvb = sb("mvb", [128, 2], F32)

        wsrc2 = sb("wsrc2", [128, 8, 128], BF16)
        ckb = sb("ckb2", [128, 4, 128], BF16)
        wstage = [pre, hf]
        WSK = [["pre0", "pre1"], ["hf"]]
        wsrc = cat[:].rearrange("p (g i) -> p g i", g=8)
        sqj = vg[:]
        sgt = [ug[:, 0:TG], vg[:, 0:TG]]
        yo = [pre, hf]
        actT = [cat[:, 0:GC * TG].rearrange("p (c t) -> p c t", c=GC), catT[:].rearrange("p a b -> p (a b)")[:, 0:GC * TG].rearrange("p (c t) -> p c t", c=GC)]
        actT.append(co[:].bitcast(BF16)[:, 0:GC * TG].rearrange("p (c t) -> p c t", c=GC))
        actT.append(vm[:].bitcast(BF16)[:, 0:GC * TG].rearrange("p (c t) -> p c t", c=GC))
        ATK = [["cat_a", "cat_c"], ["catT"], ["co"], ["vm"]]
        YOK = [["pre0", "pre1"], ["hf"]]
        SGK = ["ug", "vg"]
        bank = [psb(f"bank{i}", [128, 512], F32) for i in range(7)]
        pTr = psb("pTr", [128, 8, 128], BF16)
        pTr2 = bank[6][:].bitcast(BF16).rearrange("p (a b) -> p a b", a=8)

        S = Sched(nc)
        A = S.add

        def load_xf(i):
            s = i % 2
            S.dma("sp", xf[s][:], x_d[i * 128:(i + 1) * 128, :], writes=[f"xf{s}"], lane=f"xf{s}")

        def load_xb(i):
            S.dma("pool", xb1[:], x_d[i * 128:(i + 1) * 128, :], writes=["xb"], lane="xb")

        def load_ln(which):
            gd, bd = (ln1_g_d, ln1_b_d) if which == 1 else (ln2_g_d, ln2_b_d)
            S.dma("sp", g1[:], gd.partition_broadcast(128), writes=["g1"], lane="g1")
            S.dma("sp", b1[:], bd.partition_broadcast(128), writes=["b1"], lane="b1")

        def small_rstd(src_ap, dst, rkeys, wkey, epst=None, epsk="c_eps"):
            epst = c_eps if epst is None else epst
            A("pool", lambda e: e.tensor_tensor(dst[:], src_ap, epst[:], op=ALU.add), reads=list(rkeys) + [epsk], writes=[wkey])
            A("pool", lambda e: e.tensor_tensor(dst[:], dst[:], c_mh[:], op=ALU.pow), reads=[wkey, "c_mh"], writes=[wkey])

        A("pool", lambda e: e.memset(c_eps[:], EPS), writes=["c_eps"])
        A("pool", lambda e: e.memset(c_eps_r[:], EPS * ALPHA * ALPHA), writes=["c_eps_r"])
        A("pool", lambda e: e.memset(c_eps_l[:], EPS / (ALPHA * ALPHA)), writes=["c_eps_l"])
        A("pool", lambda e: e.memset(c_mh[:], -0.5), writes=["c_mh"])
        S.dma("pool", ident[:], ident_d, writes=["ident"], lane="ident")
        load_xb(0)
        load_xf(0)
        w_in_v = w_in_d.rearrange("(k p) n -> p k n", p=128)
        for k in range(8):
            S.dma("pool", w_in[:, k, 0:768], w_in_v[:, k, 0:768], writes=[f"w_inA{k}"], lane=f"w_inA{k}")
        for k in range(8):
            S.dma("pool", w_in[:, k, 768:D_IN], w_in_v[:, k, 768:D_IN], writes=[f"w_inB{k}"], lane=f"w_inB{k}")
        S.dma("sp", rope[:].rearrange("p a b -> p (a b)"), rope_d, writes=["rope"], lane="rope")
        S.dma("sp", maskc[:], mask_d, writes=["maskc"], lane="maskc")
        S.dma("sp", gv[:], lnv_g_d.partition_broadcast(128), writes=["gv"], lane="gv")
        S.dma("sp", bv[:], lnv_b_d.partition_broadcast(128), writes=["bv"], lane="bv")
        S.dma("sp", esink[:], sinks_d.partition_broadcast(128), writes=["esink"], lane="esink")
        A("act", lambda e: e.activation(esink[:], esink[:], AF.Exp), reads=["esink"], writes=["esink"])
        S.dma("sp", bT[:], bsp_d.rearrange("g i -> i g"), writes=["bT"], lane="bT", allow_slow_non_contiguous=True)
        S.dma("sp", bTs[0:64, :], bsp_d[:, 0:64].rearrange("g i -> i g"), writes=["bTs"], lane="bTs", allow_slow_non_contiguous=True)
        S.dma("sp", bTs[64:128, :], bsp_d[:, 0:64].rearrange("g i -> i g"), writes=["bTs"], lane="bTs", allow_slow_non_contiguous=True)
        S.dma("sp", gcat[:, 0:4], ng_a_d.rearrange("o (k p) -> p (o k)", p=128), writes=["gcat"], lane="gcat", allow_slow_non_contiguous=True)
        S.dma("sp", gcat[:, 4:8], ng_c_d.rearrange("o (k p) -> p (o k)", p=128), writes=["gcat"], lane="gcat", allow_slow_non_contiguous=True)
        load_xf(1)
        for i in range(3):
            A("pool", lambda e, i=i: e.memset(vaug[i][:], 1.0), writes=[f"vaug{i}"])
        S.dma("pool", wsrc[:], wsp_d.rearrange("g i j -> i g j"), writes=["cat_a", "cat_c"], lane="wsrc")

        A("pool", lambda e: e.memset(wsrc2[:], 0.0), writes=["wsrc2a", "wsrc2b"])
        S.dma("pool", wsrc2[0:64, :, 0:64], wsp_d[:, 0:64, 0:64].rearrange("g i j -> i g j"), reads=["wsrc2a"], writes=["wsrc2a"], lane="wsrc2a")
        S.dma("pool", wsrc2[64:128, :, 64:128], wsp_d[:, 0:64, 0:64].rearrange("g i j -> i g j"), reads=["wsrc2b"], writes=["wsrc2b"], lane="wsrc2b")
        S.dma("pool", ckb[:], ck_d.rearrange("s t f -> t s f"), writes=["ckb"], lane="ckb")
        A("pool", lambda e: e.memset(vaug_c[:], 1.0), writes=[f"vaug_c{s4}" for s4 in range(4)])
        for s4 in range(4):
            S.dma("pool", vaug_c[:, s4, :, 0:64], cv_d[s4].rearrange("t (g d) -> t g d", g=2), reads=[f"vaug_c{s4}"], writes=[f"vaug_c{s4}"], lane=f"vaug_c{s4}")

        def setup_part2():
            S.tag = "setup2"
            for g in range(8):
                A("pe", lambda e, g=g: e.transpose(pTr[:, g, :], wsrc[:, g, :], ident[:]), reads=["cat_a", "cat_c", "ident"], writes=["pTr"])
            A("dve", lambda e: e.tensor_tensor(WT[:], pTr[:], maskc[:].unsqueeze(1).to_broadcast([128, 8, 128]), op=ALU.mult),
              reads=["pTr", "maskc"], writes=["WT"])
            w_out_v = w_out_d.rearrange("(k p) n -> p k n", p=128)
            for k in range(8):
                S.dma("pool", w_out[:, k, :], w_out_v[:, k, :], writes=[f"w_out{k}"], lane=f"w_out{k}")

        def setup_part2b(ks=range(8)):
            S.tag = "setup2b"
            for k in ks:
                A("act", lambda e, k=k: e.activation(w_out[:, k, :], w_out[:, k, :], AF.Copy, scale=gcat[:, k:k + 1]),
                  reads=[f"w_out{k}", "gcat"], writes=[f"w_out{k}"])

        def setup_part2c():
            S.tag = "setup2c"
            for g in range(8):
                A("pe", lambda e, g=g: e.transpose(pTr[:, g, :], wsrc2[:, g, :], ident[:]), reads=["wsrc2a", "wsrc2b", "ident"], writes=["pTr"])
            A("dve", lambda e: e.tensor_copy(WTs[:], pTr[:]), reads=["pTr"], writes=["WTs"])
            for s4 in range(4):
                A("pe", lambda e, s4=s4: e.transpose(pTr[:, s4, :], ckb[:, s4, :], ident[:]), reads=["ckb", "ident"], writes=["pTr"])
            A("dve", lambda e: e.tensor_copy(kTc[:], pTr[:, 0:4, :]), reads=["pTr"], writes=["kTc"])

        w_gu_v = w_gu_d.rearrange("(k p) n -> p k n", p=128)

        def load_group(G, parts=(0, 1, 2)):
            g = G % NG
            s = G % NSLOT
            c0 = g * GC * 128
            key = f"ring{s}"
            if 0 in parts:
                S.dma("pool", ring_g[s][:], w_gu_v[:, :, c0:c0 + GC * 128], writes=[key], lane=key)
            if 1 in parts:
                S.dma("pool", ring_u[s][:], w_gu_v[:, :, D_FF + c0:D_FF + c0 + GC * 128], writes=[key], lane=key)
            if 2 in parts:
                S.dma("pool", ring_d[s][:], w_dn_d[c0:c0 + GC * 128, :].rearrange("(c p) n -> p c n", p=128), writes=[key], lane=key)

        def tinfo(ti):
            kind, j = TILES[ti]
            is_s = kind == "s"
            return dict(kind=kind, j=j, is_s=is_s, ridx=16 if is_s else j, out_tile=is_s or j == 15,
                        lt=ti % TPS, xs=ti % 2, kv=2 if is_s else (j % 2))

        def stage_A1a(ti):
            S.tag = "A1a(%d)" % ti
            for k in range(8):
                A("pe", lambda e, k=k: e.transpose(pTr[:, k, :], xb1[:, k * 128:(k + 1) * 128], ident[:]), reads=["xb", "ident"], writes=["pTr"])
            A("dve", lambda e: e.tensor_copy(xT[:], pTr[:]), reads=["pTr"], writes=["xT"])
            if ti + 1 < NT:
                load_xb(ti + 1)

        def stage_A1(ti):
            S.tag = "A1(%d)" % ti
            I = tinfo(ti)
            j, is_s, kv = I["j"], I["is_s"], I["kv"]
            xs_ = I["xs"]
            for (b, c0, w) in ((2, 0, 512), (3, 512, 256)):
                for k in range(8):
                    A("pe", lambda e, b=b, c0=c0, w=w, k=k: e.matmul(bank[b][:, 0:w], xT[:, k, :], w_in[:, k, c0:c0 + w], start=(k == 0), stop=(k == 7)),
                      reads=["xT", f"w_inA{k}"], writes=[f"bank{b}"])
            q_nat = bank[2][:].rearrange("p (g c d) -> p c g d", g=2, c=4)
            A("act", lambda e: e.activation(qkb[:, 0:512].rearrange("p (c g d) -> p c g d", c=4, g=2), q_nat, AF.Copy), reads=["bank2"], writes=["qkb"])
            A("act", lambda e: e.activation(xr[:, 0:8, :].rearrange("p (c g) r -> p c g r", c=4), q_nat[:, :, :, 0:16], AF.Copy), reads=["bank2"], writes=["xr"])
            k_nat = bank[3][:, 0:128].rearrange("p (g d) -> p g d", g=2)
            A("act", lambda e: e.activation(qkb[:, 512:640], bank[3][:, 0:128], AF.Copy), reads=["bank3", "qkb"], writes=["qkb"])
            A("act", lambda e: e.activation(xr[:, 8:10, :], k_nat[:, :, 0:16], AF.Copy), reads=["bank3", "xr"], writes=["xr"])
            A("act", lambda e: e.activation(vaug[kv][:, :, 0:64], bank[3][:, 128:256].rearrange("p (g d) -> p g d", g=2), AF.Copy),
              reads=["bank3"], writes=[f"vaug{kv}"])
            if I["out_tile"]:
                A("act", lambda e: e.activation(kout[:], bank[3][:, 0:128], AF.Copy), reads=["bank3"], writes=["kout"])
                A("act", lambda e: e.activation(vf[:], bank[3][:, 128:256], AF.Copy), reads=["bank3"], writes=["vf"])
            qb3 = qkb[:].rearrange("p (h d) -> p h d", h=10)
            ridx = I["ridx"]
            cosb = rope[:, ridx, 0:8].unsqueeze(1).to_broadcast([128, 10, 8])
            sinb = rope[:, ridx, 8:16].unsqueeze(1).to_broadcast([128, 10, 8])
            A("dve", lambda e: e.tensor_tensor(rt[0][:], xr[:, :, 0:8], cosb, op=ALU.mult), reads=["xr", "rope"], writes=["rt0"])
            A("dve", lambda e: e.tensor_tensor(rt[1][:], xr[:, :, 8:16], sinb, op=ALU.mult), reads=["xr", "rope"], writes=["rt1"])
            A("dve", lambda e: e.tensor_tensor(rt[2][:], xr[:, :, 8:16], cosb, op=ALU.mult), reads=["xr", "rope"], writes=["rt2"])
            A("dve", lambda e: e.tensor_tensor(rt[3][:], xr[:, :, 0:8], sinb, op=ALU.mult), reads=["xr", "rope"], writes=["rt3"])
            A("dve", lambda e: e.tensor_tensor(qb3[:, :, 0:8], rt[0][:], rt[1][:], op=ALU.subtract), reads=["rt0", "rt1", "qkb"], writes=["qkb"])
            A("dve", lambda e: e.tensor_tensor(qb3[:, :, 8:16], rt[2][:], rt[3][:], op=ALU.add), reads=["rt2", "rt3", "qkb"], writes=["qkb"])
            if I["out_tile"]:
                ko3 = kout[:].rearrange("p (h d) -> p h d", h=2)
                A("pool", lambda e: e.tensor_tensor(ko3[:, :, 0:8], rt[0][:, 8:10, :], rt[1][:, 8:10, :], op=ALU.subtract), reads=["rt0", "rt1", "kout"], writes=["kout"])
                A("pool", lambda e: e.tensor_tensor(ko3[:, :, 8:16], rt[2][:, 8:10, :], rt[3][:, 8:10, :], op=ALU.add), reads=["rt2", "rt3", "kout"], writes=["kout"])
                if is_s:
                    S.dma("sp", ks_d[j * 128:(j + 1) * 128, :], kout[:], reads=["kout"], lane="o_k")
                    S.dma("sp", vs_d[j * 128:(j + 1) * 128, :], vf[:], reads=["vf"], lane="o_v")
                else:
                    S.dma("sp", kp_d, kout[:], reads=["kout"], lane="o_k")
                    S.dma("sp", vp_d, vf[:], reads=["vf"], lane="o_v")

        def stage_A2(ti):
            S.tag = "A2(%d)" % ti
            I = tinfo(ti)
            j, is_s = I["j"], I["is_s"]
            for (b, c0) in ((4, 768), (5, 1280)):
                for k in range(8):
                    A("pe", lambda e, b=b, c0=c0, k=k: e.matmul(bank[b][:], xT[:, k, :], w_in[:, k, c0:c0 + 512], start=(k == 0), stop=(k == 7)),
                      reads=["xT", f"w_inB{k}"], writes=[f"bank{b}"])

        def stage_A2y(ti):
            S.tag = "A2y(%d)" % ti
            I = tinfo(ti)
            j, is_s = I["j"], I["is_s"]
            A("act", lambda e: e.activation(ug[:], bank[4][:], AF.Gelu_apprx_tanh), reads=["bank4"], writes=["ug"])
            A("act", lambda e: e.activation(vg[:], bank[5][:], AF.Gelu_apprx_tanh), reads=["bank5"], writes=["vg"])

        def stage_A2v(ti):
            S.tag = "A2v(%d)" % ti
            I = tinfo(ti)
            j, is_s = I["j"], I["is_s"]
            A("dve", lambda e: e.bn_stats(stt_v[:], vg[:]), reads=["vg"], writes=["stt_v"])
            A("dve", lambda e: e.bn_aggr(mv_v[:], stt_v[:]), reads=["stt_v"], writes=["mv_v"])
            small_rstd(mv_v[:, 1:2], sm["v"], ["mv_v"], "sm_v")
            A("dve", lambda e: e.scalar_tensor_tensor(vm[:], vg[:], mv_v[:, 0:1], gv[:], op0=ALU.subtract, op1=ALU.mult), reads=["vg", "mv_v", "gv"], writes=["vm"])
            if is_s:
                A("dve", lambda e: e.scalar_tensor_tensor(vm[:], vm[:], sm["v"][:, 0:1], bv[:], op0=ALU.mult, op1=ALU.add), reads=["vm", "sm_v", "bv"], writes=["vm"])
                A("pool", lambda e: e.tensor_copy(vmb[:], vm[:]), reads=["vm"], writes=["vmb"])
                S.dma("sp", ms_d[j * 128:(j + 1) * 128, :], vm[:], reads=["vm"], lane="o_m")
            else:
                A("dve", lambda e: e.scalar_tensor_tensor(vmb[:], vm[:], sm["v"][:, 0:1], bv[:], op0=ALU.mult, op1=ALU.add), reads=["vm", "sm_v", "bv"], writes=["vmb"])

        def stage_E(ti):
            S.tag = "E(%d)" % ti
            lt = tinfo(ti)["lt"]
            A("act", lambda e: e.activation(hb[:], acc[:, lt, :], AF.Copy), reads=[f"acc{lt}_0", f"acc{lt}_1"], writes=["hb"])
            for k in range(8):
                A("pe", lambda e, k=k: e.transpose(pTr[:, k, :], hb[:, k * 128:(k + 1) * 128], ident[:]), reads=["hb", "ident"], writes=["pTr"])
            A("act", lambda e: e.activation(hT[:, :, lt * 128:(lt + 1) * 128], pTr[:], AF.Copy), reads=["pTr"], writes=[f"hT{lt // 3}"])

        def key_tiles(I):
            j = I["j"]
            if I["is_s"]:
                sa, sb_ = 2 * j, 2 * j + 1
                return [
                    (lambda g: kTc[g * 64:(g + 1) * 64, sa, :], "kTc", (0, 64), lambda g: vaug_c[:, sa, g, :], f"vaug_c{sa}", [((0, 128), (64, 128))]),
                    (lambda g: kTc[g * 64:(g + 1) * 64, sb_, :], "kTc", (64, 128), lambda g: vaug_c[:, sb_, g, :], f"vaug_c{sb_}", [((0, 128), (0, 64))]),
                    (lambda g: kT[2][g * 64:(g + 1) * 64, :], "kT2", (0, 128), lambda g: vaug[2][:, g, :], "vaug2",
                     [((0, 64), (64, 128)), ((64, 128), (0, 64))]),
                ]
            cur, prv = j % 2, (j - 1) % 2
            kts = []
            if j > 0:
                kts.append((lambda g: kT[prv][g * 64:(g + 1) * 64, :], f"kT{prv}", (0, 128), lambda g: vaug[prv][:, g, :], f"vaug{prv}", [((0, 64), (64, 128))]))
            kts.append((lambda g: kT[cur][g * 64:(g + 1) * 64, :], f"kT{cur}", (0, 128), lambda g: vaug[cur][:, g, :], f"vaug{cur}", [((64, 128), (0, 64))]))
            return kts

        def stage_B(ti):
            S.tag = "B(%d)" % ti
            I = tinfo(ti)
            kv = I["kv"]
            for c in range(5):
                A("pe", lambda e, c=c: e.transpose(pTr[:, c, :], qkb[:, c * 128:(c + 1) * 128], ident[:]), reads=["qkb", "ident"], writes=["pTr"])

        def stage_Bc(ti):
            S.tag = "Bc(%d)" % ti
            kv = tinfo(ti)["kv"]
            A("act", lambda e: e.activation(qT[:], pTr[:, 0:4, :], AF.Copy), reads=["pTr"], writes=["qT"])
            A("act", lambda e: e.activation(kT[kv][:], pTr[:, 4, :], AF.Copy), reads=["pTr"], writes=[f"kT{kv}"])
            A("act", lambda e: e.activation(dmy[:], c_mh[:], AF.Exp), reads=["c_mh", "dmy"], writes=["dmy"])

        def stage_B2(ti):
            S.tag = "B2(%d)" % ti
            I = tinfo(ti)
            bi = 0
            for g in range(2):
                for kt, (kget, kkey, (q0, q1), vget, vkey, zb) in enumerate(key_tiles(I)):
                    b = bi % 4
                    bi += 1
                    nq = q1 - q0
                    pS = bank[b][:, 0:4 * nq].rearrange("p (c q) -> p c q", c=4)
                    A("pe", lambda e, pS=pS, kget=kget, g=g, q0=q0, q1=q1: e.matmul(pS, kget(g), qT[g * 64:(g + 1) * 64, :, q0:q1], start=True, stop=True),
                      reads=[kkey, "qT"], writes=[f"bank{b}"])
                    pt = PT[g][kt]
                    ptk = f"PT{g}{kt}"
                    A("act", lambda e, pt=pt, pS=pS, q0=q0, q1=q1: e.activation(pt[:, :, q0:q1], pS, AF.Exp, scale=0.125), reads=[f"bank{b}"], writes=[ptk])
                    for ((p0, p1), (z0, z1)) in zb:
                        A("pool", lambda e, pt=pt, p0=p0, p1=p1, z0=z0, z1=z1: e.memset(pt[p0:p1, :, z0:z1], 0.0), reads=[ptk], writes=[ptk])

            A("act", lambda e: e.activation(dmy[:], c_mh[:], AF.Gelu_apprx_tanh), reads=["c_mh"], writes=["dmy"])

        def stage_C(ti):
            S.tag = "C(%d)" % ti
            I = tinfo(ti)
            is_s = I["is_s"]
            kts = key_tiles(I)
            nkt = len(kts)
            Wt = WTs if is_s else WT
            bt = bTs if is_s else bT
            pG = bank[6][:].rearrange("p (g d) -> p g d", g=8)
            for g in range(8):
                A("pe", lambda e, g=g, Wt=Wt: e.matmul(pG[:, g, :], Wt[:, g, :], vmb[:, g * 64:(g + 1) * 64], start=True, stop=True),
                  reads=["WT", "WTs", "vmb"], writes=["bank6"])
            co3 = co[:].rearrange("p (g d) -> p g d", g=8)
            A("dve", lambda e, bt=bt: e.tensor_tensor(co3, pG, bt[:].unsqueeze(2).to_broadcast([128, 8, 64]), op=ALU.add), reads=["bank6", "bT", "bTs"], writes=["co"])
            A("pool", lambda e: e.tensor_tensor(cat[:, 512:1024], co[:], ug[:], op=ALU.mult), reads=["co", "ug"], writes=["cat_c"])
            for g in range(2):
                pO = bank[g][:, 0:260].rearrange("p (c d) -> p c d", c=4)
                for c in range(4):
                    for kt, (kget, kkey, (q0, q1), vget, vkey, zb) in enumerate(kts):
                        A("pe", lambda e, pO=pO, g=g, c=c, kt=kt, vget=vget: e.matmul(pO[:, c, :], PT[g][kt][:, c, :], vget(g), start=(kt == 0), stop=(kt == nkt - 1)),
                          reads=[f"PT{g}{kt}", vkey], writes=[f"bank{g}"])
            for g in range(2):
                pO = bank[g][:, 0:260].rearrange("p (c d) -> p c d", c=4)
                A("dve", lambda e, pO=pO, g=g: e.tensor_tensor(den[:, g * 4:(g + 1) * 4], pO[:, :, 64], esink[:, g * 4:(g + 1) * 4], op=ALU.add),
                  reads=[f"bank{g}", "esink"], writes=[f"den{g}"])
            A("dve", lambda e: e.reciprocal(rden[:], den[:]), reads=["den0", "den1"], writes=["rden"])
            for g in range(2):
                pO = bank[g][:, 0:260].rearrange("p (c d) -> p c d", c=4)
                ca3 = cat[:, g * 256:(g + 1) * 256].rearrange("p (c d) -> p c d", c=4)
                A("dve", lambda e, pO=pO, ca3=ca3, g=g: e.tensor_tensor(ca3, pO[:, :, 0:64], rden[:, g * 4:(g + 1) * 4].unsqueeze(2).to_broadcast([128, 4, 64]), op=ALU.mult),
                  reads=[f"bank{g}", "rden"], writes=["cat_a"])

        def stage_C2(ti):
            S.tag = "C2(%d)" % ti
            A("act", lambda e: e.activation(sqj, cat[:, 0:512], AF.Square, scale=float(ALPHA * 512 ** -0.5), accum_out=sm["a1"][:, 0:1]), reads=["cat_a"], writes=["vg", "sm_a1"])
            small_rstd(sm["a1"][:, 0:1], sm["a2"], ["sm_a1"], "sm_a2", c_eps_r, "c_eps_r")
            A("act", lambda e: e.activation(sqj, cat[:, 512:1024], AF.Square, scale=float(ALPHA * 512 ** -0.5), accum_out=sm["c1"][:, 0:1]), reads=["cat_c"], writes=["vg", "sm_c1"])
            small_rstd(sm["c1"][:, 0:1], sm["c2"], ["sm_c1"], "sm_c2", c_eps_r, "c_eps_r")

        def stage_D(ti):
            S.tag = "D(%d)" % ti
            I = tinfo(ti)
            lt, xs = I["lt"], I["xs"]
            for k in range(8):
                A("pe", lambda e, k=k: e.transpose(pTr2[:, k, :], cat[:, k * 128:(k + 1) * 128], ident[:]), reads=["cat_a", "cat_c", "ident"], writes=["bank6"])
            A("dve", lambda e: e.tensor_copy(catT[:], pTr2), reads=["bank6"], writes=["catT"])

        def stage_D1b(ti):
            S.tag = "D1b(%d)" % ti
            I = tinfo(ti)
            lt, xs = I["lt"], I["xs"]
            for (kb, b0) in ((0, 4), (4, 2)):
                for n in range(2):
                    for k in range(kb, kb + 4):
                        A("pe", lambda e, n=n, k=k, kb=kb, b0=b0: e.matmul(bank[b0 + n][:], catT[:, k, :], w_out[:, k, n * 512:(n + 1) * 512], start=(k == kb), stop=(k == kb + 3)),
                          reads=["catT", f"w_out{k}"], writes=[f"bank{b0 + n}"])
            for n in range(2):
                cs = slice(n * 512, (n + 1) * 512)
                A("dve", lambda e, n=n, cs=cs: e.scalar_tensor_tensor(pre[:, cs], bank[4 + n][:], sm["a2"][:, 0:1], xf[xs][:, cs], op0=ALU.mult, op1=ALU.add),
                  reads=[f"xf{xs}", f"bank{4 + n}", "sm_a2"], writes=[f"pre{n}"])
                A("dve", lambda e, n=n, cs=cs: e.scalar_tensor_tensor(pre[:, cs], bank[2 + n][:], sm["c2"][:, 0:1], pre[:, cs], op0=ALU.mult, op1=ALU.add),
                  reads=[f"pre{n}", f"bank{2 + n}", "sm_c2"], writes=[f"pre{n}"])
                A("dve", lambda e, n=n, cs=cs: e.bn_stats(stt[:, 6 * n:6 * n + 6], pre[:, cs]), reads=[f"pre{n}"], writes=["stt"])
            if ti + 2 < NT:
                load_xf(ti + 2)
            A("dve", lambda e: e.bn_aggr(mv[:], stt[:]), reads=["stt"], writes=["mv"])
            small_rstd(mv[:, 1:2], sm["l1"], ["mv"], "sm_l1", c_eps_l, "c_eps_l")

        def stage_D2(ti):
            S.tag = "D2(%d)" % ti
            lt = tinfo(ti)["lt"]
            A("dve", lambda e: e.scalar_tensor_tensor(hf[:], pre[:], mv[:, 0:1], g1[:], op0=ALU.subtract, op1=ALU.mult), reads=["pre0", "pre1", "mv", "g1"], writes=["hf"])
            A("dve", lambda e: e.scalar_tensor_tensor(acc[:, lt, :], hf[:], sm["l1"][:, 0:1], b1[:], op0=ALU.mult, op1=ALU.add), reads=["hf", "sm_l1", "b1"], writes=[f"acc{lt}_0", f"acc{lt}_1"])

        def phase_a(sg):
            t0 = sg * TPS
            import os
            KSTOP = int(os.environ.get("KSTOP", "-1"))
            if sg == 0:
                stage_A1a(t0)
            stage_A1(t0)
            stage_A2(t0)
            if sg == 0:
                setup_part2()
            for it in range(TPS + 2):
                if KSTOP >= 0 and it >= KSTOP:
                    break
                t = t0 + it
                nxt = it + 1 < TPS
                if it < TPS:
                    stage_B(t)
                    stage_A2y(t)
                    stage_Bc(t)
                if 1 <= it <= TPS:
                    stage_D(t - 1)
                if nxt:
                    stage_A1a(t + 1)
                if it < TPS:
                    stage_A2v(t)
                    stage_B2(t)
                    if sg == 0 and it == 0:
                        setup_part2b(range(0, 4))
                    if sg == 0 and it == 1:
                        setup_part2c()
                if 1 <= it <= TPS:
                    stage_D1b(t - 1)
                if it < TPS:
                    stage_C(t)
                if 2 <= it:
                    if it == TPS + 1:
                        DEFER_E.append(t - 2)
                    else:
                        stage_E(t - 2)
                if nxt:
                    stage_A1(t + 1)
                if sg == 0 and it == 0:
                    setup_part2b(range(4, 8))
                if 1 <= it <= TPS:
                    stage_D2(t - 1)
                if it < TPS:
                    stage_C2(t)
                if nxt:
                    stage_A2(t + 1)
                if sg > 0 and it < 3:
                    for Gd in LATE:
                        load_group(Gd, parts=(it,))
                if sg == 0 and 2 <= it <= 7 and KSTOP < 0:
                    load_group((it - 2) // 3, parts=((it - 2) % 3,))

        OUTB = [bank[4][:], bank[5][:], bank[6][:], pTr[:].rearrange("p a b -> p (a b)").bitcast(F32)]
        OUTK = ["bank4", "bank5", "bank6", "pTr"]

        def phase_b(sg):
            it = 0
            passes = [[g] for g in range(NG - 2)] + [[NG - 2, NG - 1]]
            pend = []
            ln_pend = []
            late_loads = []

            def flush(parts):
                for fn in list(parts):
                    fn()

            def gate_up(t, slot, rkey, cc, ab):
                bg, bu = bank[2 * cc], bank[2 * cc + 1]
                for k in range(8):
                    A("pe", lambda e, k=k: e.matmul(bg[:, 0:TG], ring_g[slot][:, k, cc * 128:(cc + 1) * 128], hT[:, k, t * TG:(t + 1) * TG], start=(k == 0), stop=(k == 7)),
                      reads=[rkey, f"hT{t}"], writes=[f"bank{2 * cc}"])
                for k in range(8):
                    A("pe", lambda e, k=k: e.matmul(bu[:, 0:TG], ring_u[slot][:, k, cc * 128:(cc + 1) * 128], hT[:, k, t * TG:(t + 1) * TG], start=(k == 0), stop=(k == 7)),
                      reads=[rkey, f"hT{t}"], writes=[f"bank{2 * cc + 1}"])
                A("act", lambda e: e.activation(sgt[cc], bg[:, 0:TG], AF.Silu), reads=[f"bank{2 * cc}"], writes=[SGK[cc]])
                A("dve", lambda e: e.tensor_tensor(actT[ab][:, cc, :], sgt[cc], bu[:, 0:TG], op=ALU.mult),
                  reads=[SGK[cc], f"bank{2 * cc + 1}"], writes=ATK[ab])

            def down(t, i, chunks, first, last):
                lt = t * (TG // 128) + i
                for n in range(2):
                    oi = (n + 2 * i) % 4
                    obk = OUTK[oi]
                    obt = OUTB[oi]
                    for ci, (ab, cc, slot, rkey) in enumerate(chunks):
                        A("pe", lambda e, obt=obt, cc=cc, n=n, ab=ab, slot=slot, ci=ci: e.matmul(obt, actT[ab][:, cc, i * 128:(i + 1) * 128], ring_d[slot][:, cc, n * 512:(n + 1) * 512], start=(ci == 0), stop=(ci == len(chunks) - 1)),
                          reads=ATK[ab] + [rkey], writes=[obk])
                    accs = acc[:, lt, n * 512:(n + 1) * 512]
                    if first:
                        A("dve", lambda e, obt=obt, accs=accs: e.scalar_tensor_tensor(accs, accs, ALPHA, obt, op0=ALU.mult, op1=ALU.add),
                          reads=[obk, f"acc{lt}_{n}"], writes=[f"acc{lt}_{n}"])
                    else:
                        A("dve", lambda e, obt=obt, accs=accs: e.tensor_tensor(accs, obt, accs, op=ALU.add),
                          reads=[obk, f"acc{lt}_{n}"], writes=[f"acc{lt}_{n}"])
                if last:
                    ti = sg * TPS + lt
                    ys = ti % 2
                    p = str(lt % 2)
                    mean_t, ex2_t, sq_t, rs_t = sm["m" + p], sm["e" + p], sm["q" + p], sm["r" + p]
                    junk = hb[:]
                    A("act", lambda e: e.activation(junk, acc[:, lt, :], AF.Identity, scale=1.0 / 1024, accum_out=mean_t[:, 0:1]),
                      reads=[f"acc{lt}_0", f"acc{lt}_1"], writes=["hb", "sm_m" + p])
                    A("act", lambda e: e.activation(junk, acc[:, lt, :], AF.Square, scale=1.0 / 32, accum_out=ex2_t[:, 0:1]),
                      reads=[f"acc{lt}_0", f"acc{lt}_1"], writes=["hb", "sm_e" + p])
                    A("pool", lambda e: e.tensor_tensor(sq_t[:], mean_t[:], mean_t[:], op=ALU.mult), reads=["sm_m" + p], writes=["sm_q" + p])
                    A("pool", lambda e: e.tensor_tensor(ex2_t[:], ex2_t[:], sq_t[:], op=ALU.subtract), reads=["sm_e" + p, "sm_q" + p], writes=["sm_e" + p])
                    small_rstd(ex2_t[:, 0:1], rs_t, ["sm_e" + p], "sm_r" + p)
                    flush(ln_pend)
                    ln_pend.clear()

                    def tail(lt=lt, ys=ys, ti=ti, p=p, mean_t=mean_t, rs_t=rs_t):
                        A("dve", lambda e: e.scalar_tensor_tensor(yo[ys][:], acc[:, lt, :], mean_t[:, 0:1], g1[:], op0=ALU.subtract, op1=ALU.mult),
                          reads=[f"acc{lt}_0", f"acc{lt}_1", "sm_m" + p, "g1"], writes=YOK[ys])
                        A("dve", lambda e: e.scalar_tensor_tensor(yo[ys][:], yo[ys][:], rs_t[:, 0:1], b1[:], op0=ALU.mult, op1=ALU.add),
                          reads=YOK[ys] + ["sm_r" + p, "b1"], writes=YOK[ys])
                        S.dma("sp", y_d[ti * 128:(ti + 1) * 128, :], yo[ys][:], reads=YOK[ys], lane=f"o_y{ys}")
                    ln_pend.append(tail)

            for gl in passes:
                first = gl[0] == 0
                last = gl[-1] == NG - 1
                merged = len(gl) > 1
                loads = []
                deferred = []
                for gi, g in enumerate(gl):
                    G = sg * NG + g
                    if G + 2 < NSG * NG:
                        (loads if gi == 0 else deferred).append(G + 2)
                for t in range(NTG):
                    chunks = []
                    specs = []
                    for gi, g in enumerate(gl):
                        G = sg * NG + g
                        slot = G % NSLOT
                        ab = (2 * ((t + 1) % 2) + gi) if merged else it % 2
                        for cc in range(GC):
                            specs.append((slot, f"ring{slot}", cc, ab))
                            chunks.append((ab, cc, slot, f"ring{slot}"))
                    it += 1
                    for si, (slot, rkey, cc, ab) in enumerate(specs):
                        gate_up(t, slot, rkey, cc, ab)
                        if DEFER_E:
                            stage_E(DEFER_E.pop())
                        if pend:
                            take = 1 if si + 1 < len(specs) else len(pend)
                            flush(pend[:take])
                            pend = pend[take:]
                    if t == 0:
                        for Gl in loads:
                            load_group(Gl)
                    parts = [lambda t=t, i=i, chunks=chunks, first=first, last=last: down(t, i, chunks, first, last) for i in range(TG // 128)]
                    pend = parts
                late_loads.extend(deferred)
            flush(pend)
            pend = []
            flush(ln_pend)
            ln_pend.clear()
            LATE.extend(late_loads)

        import os
        KSTOP = int(os.environ.get("KSTOP", "-1"))
        LATE = []
        DEFER_E = []
        for sg in range(NSG):
            load_ln(1)
            phase_a(sg)
            if KSTOP >= 0:
                break
            load_ln(2)
            if sg + 1 < NSG:
                stage_A1a((sg + 1) * TPS)
            phase_b(sg)

        S.emit(st, final_wait_lanes=["o_k", "o_v", "o_m", "o_y0", "o_y1"])
    return nc


_NC_CACHE = {}


def _consts():
    half = 8
    inv = np.power(np.float32(500000.0), -np.arange(half, dtype=np.float32) * np.float32(2.0 / 16)).astype(np.float32)
    rope = np.zeros((128, 17, 16), np.float32)
    p = np.arange(128)
    for t in range(17):
        pos = (128 * t + p) if t < 16 else (1024 + (p % 64))
        ang = pos.astype(np.float32)[:, None] * inv[None, :]
        rope[:, t, 0:8] = np.cos(ang)
        rope[:, t, 8:16] = np.sin(ang)
    i = np.arange(128)
    mask = ((i[:, None] // 64) <= (i[None, :] // 64)).astype(np.float32)
    return np.eye(128, dtype=np.float32), rope.reshape(128, 17 * 16), mask


def kernel(x_prompt, x_sample, cache_win_k, cache_win_v, w_in, ln_v_g, ln_v_b, attn_sinks, w_spatial, b_spatial,
           norm_attn_g, norm_cmlp_g, w_out, ln1_g, ln1_b, w_gate_up, w_down, ln2_g, ln2_b):
    f = lambda a: np.ascontiguousarray(np.asarray(a, dtype=np.float32))
    x_prompt, x_sample = f(x_prompt), f(x_sample)
    ck, cv = f(cache_win_k)[0], f(cache_win_v)[0]
    if "nc" not in _NC_CACHE:
        _NC_CACHE["nc"] = build_program()
    nc = _NC_CACHE["nc"]
    ident, rope, mask = _consts()
    shared = {
        "w_in": f(w_in)[0], "w_out": f(w_out)[0], "w_gu": f(w_gate_up)[0], "w_dn": f(w_down)[0],
        "ln_v_g": f(ln_v_g), "ln_v_b": f(ln_v_b), "sinks": f(attn_sinks), "w_sp": f(w_spatial)[0], "b_sp": f(b_spatial)[0],
        "ng_a": f(norm_attn_g), "ng_c": f(norm_cmlp_g), "ln1_g": f(ln1_g), "ln1_b": f(ln1_b), "ln2_g": f(ln2_g), "ln2_b": f(ln2_b),
        "c_ident": ident, "c_rope": rope, "c_mask": mask,
    }
    in_maps = []
    for c in range(N_CORES):
        xs = x_sample[4 * c:4 * c + 4].reshape(256, D)
        xp = x_prompt[c]
        xc = np.concatenate([xp[0:1024], xs[0:128], xp[1024:2048], xs[128:256]], axis=0)
        m = dict(shared)
        m["x"] = np.ascontiguousarray(xc)
        m["cache_k"] = np.ascontiguousarray(ck[4 * c:4 * c + 4].reshape(4, 128, 128))
        m["cache_v"] = np.ascontiguousarray(cv[4 * c:4 * c + 4].reshape(4, 128, 128))
        in_maps.append(m)
    res = run_bass_kernel_spmd(nc, in_maps, core_ids=list(range(N_CORES)))
    R = res.results
    y_p = np.stack([np.concatenate([R[c]["y"][0:1024], R[c]["y"][1152:2176]], axis=0) for c in range(N_CORES)])
    y_s = np.concatenate([np.concatenate([R[c]["y"][1024:1152], R[c]["y"][2176:2304]], axis=0).reshape(4, 64, D) for c in range(N_CORES)])
    kp = np.stack([R[c]["kp"].reshape(128, 2, 64) for c in range(N_CORES)])[None]
    vp = np.stack([R[c]["vp"].reshape(128, 2, 64) for c in range(N_CORES)])[None]
    ks = np.concatenate([R[c]["ks"].reshape(4, 64, 2, 64) for c in range(N_CORES)])[None]
    vs = np.concatenate([R[c]["vs"].reshape(4, 64, 2, 64) for c in range(N_CORES)])[None]
    ms = np.concatenate([R[c]["ms"].reshape(4, 64, 8, 64) for c in range(N_CORES)])[None]
    return (y_p.astype(np.float32), y_s.astype(np.float32), kp.astype(np.float32), vp.astype(np.float32),
            ks.astype(np.float32), vs.astype(np.float32), ms.astype(np.float32))
```

```python
import numpy as np
from contextlib import ExitStack
import concourse.bass as bass
import concourse.mybir as mybir
from concourse.bass_utils import run_bass_kernel_spmd

F32 = mybir.dt.float32
BF16 = mybir.dt.bfloat16
AF = mybir.ActivationFunctionType
ALU = mybir.AluOpType

N_CORES = 8
D = 1024
D_IN = 1792
D_FF = 2816
NT = 18
TPS = 9
NSG = 2
GC = 2
NG = D_FF // (128 * GC)
NSLOT = 3
TG = 384
NTG = TPS * 128 // TG
ALPHA = 2.0 ** 0.25
EPS = 1e-5
TILES = [("p", j) for j in range(8)] + [("s", 0)] + [("p", j) for j in range(8, 16)] + [("s", 1)]


class Op:
    __slots__ = ("eng", "fn", "reads", "writes", "dma", "lane", "idx", "sig", "deps", "tag")

    def __init__(self, eng, fn, reads, writes, dma, lane):
        self.eng, self.fn = eng, fn
        self.reads, self.writes = tuple(reads), tuple(writes)
        self.dma, self.lane = dma, lane
        self.sig = None
        self.deps = []
        self.tag = None


class Sched:
    ENGS = ("pe", "act", "dve", "pool", "sp")

    def __init__(self, nc):
        self.nc = nc
        self.ops = []

    tag = ""

    def add(self, eng, fn, reads=(), writes=()):
        op = Op(eng, fn, reads, writes, False, None)
        op.tag = self.tag + " w=" + ",".join(map(str, writes))
        self.ops.append(op)
        return op

    def dma(self, queue, out, in_, reads=(), writes=(), lane=None, **kw):
        fn = lambda e, out=out, in_=in_, kw=kw: e.dma_start(out=out, in_=in_, **kw)
        op = Op(queue, fn, reads, writes, True, lane)
        op.tag = self.tag + " dma " + str(lane)
        self.ops.append(op)
        return op

    def _analyze(self):
        last_w, readers = {}, {}
        for i, op in enumerate(self.ops):
            op.idx = i
            deps = {}
            for k in op.reads:
                w = last_w.get(k)
                if w is not None:
                    deps[w.idx] = (w, True)
            for k in op.writes:
                w = last_w.get(k)
                if w is not None and w.idx not in deps:
                    deps[w.idx] = (w, True)
                rd = readers.get(k)
                if rd is not None:
                    for r in list(rd[0].values()) + rd[1]:
                        if r.idx not in deps and r is not op:
                            deps[r.idx] = (r, False)
            need = []
            for (p, raw) in deps.values():
                if p.dma or p.eng != op.eng:
                    need.append(p)
                elif raw and (p.eng != "pe" or op.dma):
                    need.append(p)
            op.deps = need
            for k in op.reads:
                rd = readers.setdefault(k, ({}, []))
                if op.dma:
                    rd[1].append(op)
                else:
                    rd[0][op.eng] = op
            for k in op.writes:
                last_w[k] = op
                readers[k] = ({}, [])
        for op in self.ops:
            for p in op.deps:
                if p.sig is None:
                    p.sig = -1
        cnt = {e: 0 for e in self.ENGS}
        lane_cnt = {}
        for op in self.ops:
            if op.dma:
                lane_cnt[op.lane] = lane_cnt.get(op.lane, 0) + 1
                op.sig = 16 * lane_cnt[op.lane]
            elif op.sig == -1:
                cnt[op.eng] += 1
                op.sig = cnt[op.eng]
        self.lane_final = {l: 16 * c for l, c in lane_cnt.items()}

    def emit(self, stack, final_wait_lanes=()):
        nc = self.nc
        self._analyze()
        sems = {}
        for e in self.ENGS:
            sems[("eng", e)] = stack.enter_context(nc.semaphore("s_" + e))
        for l in self.lane_final:
            sems[("lane", l)] = stack.enter_context(nc.semaphore("l_" + str(l)))
        block = stack.enter_context(nc.Block())
        per_eng = {e: [op for op in self.ops if op.eng == e] for e in self.ENGS}

        def body(ename, engine):
            waited = {}
            for op in per_eng[ename]:
                for p in op.deps:
                    key = ("lane", p.lane) if p.dma else ("eng", p.eng)
                    if waited.get(key, 0) < p.sig:
                        engine.wait_ge(sems[key], p.sig)
                        waited[key] = p.sig
                inst = op.fn(engine)
                if op.dma:
                    inst.then_inc(sems[("lane", op.lane)], 16)
                elif op.sig is not None:
                    inst.then_inc(sems[("eng", ename)], 1)
            if ename == "sp":
                for l in final_wait_lanes:
                    if l in self.lane_final:
                        engine.wait_ge(sems[("lane", l)], self.lane_final[l])

        block.tensor(lambda e: body("pe", e))
        block.scalar(lambda e: body("act", e))
        block.vector(lambda e: body("dve", e))
        block.gpsimd(lambda e: body("pool", e))
        block.sync(lambda e: body("sp", e))


def build_program():
    nc = bass.Bass("TRN2", target_bir_lowering=False)

    def din(name, shape):
        return nc.dram_tensor(name, list(shape), F32, kind="ExternalInput").ap()

    def dout(name, shape):
        return nc.dram_tensor(name, list(shape), F32, kind="ExternalOutput").ap()

    x_d = din("x", [NT * 128, D])
    ck_d = din("cache_k", [4, 128, 128])
    cv_d = din("cache_v", [4, 128, 128])
    w_in_d = din("w_in", [D, D_IN])
    w_out_d = din("w_out", [D, D])
    w_gu_d = din("w_gu", [D, 2 * D_FF])
    w_dn_d = din("w_dn", [D_FF, D])
    lnv_g_d = din("ln_v_g", [1, 512])
    lnv_b_d = din("ln_v_b", [1, 512])
    sinks_d = din("sinks", [1, 8])
    wsp_d = din("w_sp", [8, 128, 128])
    bsp_d = din("b_sp", [8, 128])
    ng_a_d = din("ng_a", [1, 512])
    ng_c_d = din("ng_c", [1, 512])
    ln1_g_d = din("ln1_g", [1, D])
    ln1_b_d = din("ln1_b", [1, D])
    ln2_g_d = din("ln2_g", [1, D])
    ln2_b_d = din("ln2_b", [1, D])
    ident_d = din("c_ident", [128, 128])
    rope_d = din("c_rope", [128, 17 * 16])
    mask_d = din("c_mask", [128, 128])

    y_d = dout("y", [NT * 128, D])
    kp_d = dout("kp", [128, 128])
    vp_d = dout("vp", [128, 128])
    ks_d = dout("ks", [256, 128])
    vs_d = dout("vs", [256, 128])
    ms_d = dout("ms", [256, 512])

    st = ExitStack()
    with st:
        def sb(name, shape, dt=F32):
            return st.enter_context(nc.sbuf_tensor(name, list(shape), dt))

        def psb(name, shape, dt=F32):
            return st.enter_context(nc.psum_tensor(name, list(shape), dt))

        w_in = sb("w_in_sb", [128, 8, D_IN], BF16)
        w_out = sb("w_out_sb", [128, 8, D], BF16)
        ring_g = [sb(f"ring_g{i}", [128, 8, GC * 128], BF16) for i in range(NSLOT)]
        ring_u = [sb(f"ring_u{i}", [128, 8, GC * 128], BF16) for i in range(NSLOT)]
        ring_d = [sb(f"ring_d{i}", [128, GC, D], BF16) for i in range(NSLOT)]
        acc = sb("acc", [128, TPS, D], F32)
        hT = sb("hT", [128, 8, TPS * 128], BF16)
        ident = sb("ident", [128, 128], BF16)
        rope = sb("rope", [128, 17, 16], F32)
        maskc = sb("maskc", [128, 128], F32)
        WT = sb("WT", [128, 8, 128], BF16)
        WTs = sb("WTs", [128, 8, 128], BF16)
        bT = sb("bT", [128, 8], F32)
        bTs = sb("bTs", [128, 8], F32)
        gv = sb("gv", [128, 512], F32)
        bv = sb("bv", [128, 512], F32)
        g1 = sb("g1", [128, D], F32)
        b1 = sb("b1", [128, D], F32)
        gcat = sb("gcat", [128, 8], F32)
        esink = sb("esink", [128, 8], F32)
        c_eps = sb("c_eps", [128, 1], F32)
        c_mh = sb("c_mh", [128, 1], F32)
        dmy = sb("dmy", [128, 1], F32)
        c_eps_r = sb("c_eps_r", [128, 1], F32)
        c_eps_l = sb("c_eps_l", [128, 1], F32)
        kTc = sb("kTc", [128, 4, 128], BF16)
        vaug_c = sb("vaug_c", [128, 4, 2, 65], BF16)

        xf = [sb(f"xf{i}", [128, D], F32) for i in range(2)]
        xb1 = sb("xb", [128, D], BF16)
        xT = sb("xT", [128, 8, 128], BF16)
        xr = sb("xr", [128, 10, 16], F32)
        qkb = sb("qkb", [128, 640], BF16)
        rt = [sb(f"rt{i}", [128, 10, 8], F32) for i in range(4)]
        kout = sb("kout", [128, 128], F32)
        vf = sb("vf", [128, 128], F32)
        vaug = [sb(f"vaug{i}", [128, 2, 65], BF16) for i in range(3)]
        kT = [sb(f"kT{i}", [128, 128], BF16) for i in range(3)]
        qT = sb("qT", [128, 4, 128], BF16)
        ug = sb("ug", [128, 512], F32)
        vg = sb("vg", [128, 512], F32)
        vm = sb("vm", [128, 512], F32)
        vmb = sb("vmb", [128, 512], BF16)
        PT = [[sb(f"PT{g}{k}", [128, 4, 128], BF16) for k in range(3)] for g in range(2)]
        den = sb("den", [128, 8], F32)
        rden = sb("rden", [128, 8], F32)
        co = sb("co", [128, 512], F32)
        cat = sb("cat", [128, D], BF16)
        catT = sb("catT", [128, 8, 128], BF16)
        pre = sb("pre", [128, D], F32)
        hf = sb("hf", [128, D], F32)
        hb = sb("hb", [128, D], BF16)
        stt_v = sb("stt_v", [128, 6], F32)
        mv_v = sb("mv_v", [128, 2], F32)
        stt = sb("stt", [128, 12], F32)
        mv = sb("mv", [128, 2], F32)
        sm = {n: sb("sm_" + n, [128, 1], F32) for n in ("v", "a1", "a2", "c1", "c2", "l1", "l2", "l1b", "l2b", "m0", "m1", "e0", "e1", "q0", "q1", "r0", "r1")}
        sttb = sb("sttb", [128, 12], F32)
        mvb = sb("mvb", [128, 2], F32)

        wsrc2 = sb("wsrc2", [128, 8, 128], BF16)
        ckb = sb("ckb2", [128, 4, 128], BF16)
        wstage = [pre, hf]
        WSK = [["pre0", "pre1"], ["hf"]]
        wsrc = cat[:].rearrange("p (g i) -> p g i", g=8)
        sqj = vg[:]
        sgt = [ug[:, 0:TG], vg[:, 0:TG]]
        yo = [pre, hf]
        actT = [cat[:, 0:GC * TG].rearrange("p (c t) -> p c t", c=GC), catT[:].rearrange("p a b -> p (a b)")[:, 0:GC * TG].rearrange("p (c t) -> p c t", c=GC)]
        actT.append(co[:].bitcast(BF16)[:, 0:GC * TG].rearrange("p (c t) -> p c t", c=GC))
        actT.append(vm[:].bitcast(BF16)[:, 0:GC * TG].rearrange("p (c t) -> p c t", c=GC))
        ATK = [["cat_a", "cat_c"], ["catT"], ["co"], ["vm"]]
        YOK = [["pre0", "pre1"], ["hf"]]
        SGK = ["ug", "vg"]
        bank = [psb(f"bank{i}", [128, 512], F32) for i in range(7)]
        pTr = psb("pTr", [128, 8, 128], BF16)
        pTr2 = bank[6][:].bitcast(BF16).rearrange("p (a b) -> p a b", a=8)

        S = Sched(nc)
        A = S.add

        def load_xf(i):
            s = i % 2
            S.dma("sp", xf[s][:], x_d[i * 128:(i + 1) * 128, :], writes=[f"xf{s}"], lane=f"xf{s}")

        def load_xb(i):
            S.dma("pool", xb1[:], x_d[i * 128:(i + 1) * 128, :], writes=["xb"], lane="xb")

        def load_ln(which):
            gd, bd = (ln1_g_d, ln1_b_d) if which == 1 else (ln2_g_d, ln2_b_d)
            S.dma("sp", g1[:], gd.partition_broadcast(128), writes=["g1"], lane="g1")
            S.dma("sp", b1[:], bd.partition_broadcast(128), writes=["b1"], lane="b1")

        def small_rstd(src_ap, dst, rkeys, wkey, epst=None, epsk="c_eps"):
            epst = c_eps if epst is None else epst
            A("pool", lambda e: e.tensor_tensor(dst[:], src_ap, epst[:], op=ALU.add), reads=list(rkeys) + [epsk], writes=[wkey])
            A("pool", lambda e: e.tensor_tensor(dst[:], dst[:], c_mh[:], op=ALU.pow), reads=[wkey, "c_mh"], writes=[wkey])

        A("pool", lambda e: e.memset(c_eps[:], EPS), writes=["c_eps"])
        A("pool", lambda e: e.memset(c_eps_r[:], EPS * ALPHA * ALPHA), writes=["c_eps_r"])
        A("pool", lambda e: e.memset(c_eps_l[:], EPS / (ALPHA * ALPHA)), writes=["c_eps_l"])
        A("pool", lambda e: e.memset(c_mh[:], -0.5), writes=["c_mh"])
        S.dma("pool", ident[:], ident_d, writes=["ident"], lane="ident")
        load_xb(0)
        load_xf(0)
        w_in_v = w_in_d.rearrange("(k p) n -> p k n", p=128)
        for k in range(8):
            S.dma("pool", w_in[:, k, 0:768], w_in_v[:, k, 0:768], writes=[f"w_inA{k}"], lane=f"w_inA{k}")
        for k in range(8):
            S.dma("pool", w_in[:, k, 768:D_IN], w_in_v[:, k, 768:D_IN], writes=[f"w_inB{k}"], lane=f"w_inB{k}")
        S.dma("sp", rope[:].rearrange("p a b -> p (a b)"), rope_d, writes=["rope"], lane="rope")
        S.dma("sp", maskc[:], mask_d, writes=["maskc"], lane="maskc")
        S.dma("sp", gv[:], lnv_g_d.partition_broadcast(128), writes=["gv"], lane="gv")
        S.dma("sp", bv[:], lnv_b_d.partition_broadcast(128), writes=["bv"], lane="bv")
        S.dma("sp", esink[:], sinks_d.partition_broadcast(128), writes=["esink"], lane="esink")
        A("act", lambda e: e.activation(esink[:], esink[:], AF.Exp), reads=["esink"], writes=["esink"])
        S.dma("sp", bT[:], bsp_d.rearrange("g i -> i g"), writes=["bT"], lane="bT", allow_slow_non_contiguous=True)
        S.dma("sp", bTs[0:64, :], bsp_d[:, 0:64].rearrange("g i -> i g"), writes=["bTs"], lane="bTs", allow_slow_non_contiguous=True)
        S.dma("sp", bTs[64:128, :], bsp_d[:, 0:64].rearrange("g i -> i g"), writes=["bTs"], lane="bTs", allow_slow_non_contiguous=True)
        S.dma("sp", gcat[:, 0:4], ng_a_d.rearrange("o (k p) -> p (o k)", p=128), writes=["gcat"], lane="gcat", allow_slow_non_contiguous=True)
        S.dma("sp", gcat[:, 4:8], ng_c_d.rearrange("o (k p) -> p (o k)", p=128), writes=["gcat"], lane="gcat", allow_slow_non_contiguous=True)
        load_xf(1)
        for i in range(3):
            A("pool", lambda e, i=i: e.memset(vaug[i][:], 1.0), writes=[f"vaug{i}"])
        S.dma("pool", wsrc[:], wsp_d.rearrange("g i j -> i g j"), writes=["cat_a", "cat_c"], lane="wsrc")

        A("pool", lambda e: e.memset(wsrc2[:], 0.0), writes=["wsrc2a", "wsrc2b"])
        S.dma("pool", wsrc2[0:64, :, 0:64], wsp_d[:, 0:64, 0:64].rearrange("g i j -> i g j"), reads=["wsrc2a"], writes=["wsrc2a"], lane="wsrc2a")
        S.dma("pool", wsrc2[64:128, :, 64:128], wsp_d[:, 0:64, 0:64].rearrange("g i j -> i g j"), reads=["wsrc2b"], writes=["wsrc2b"], lane="wsrc2b")
        S.dma("pool", ckb[:], ck_d.rearrange("s t f -> t s f"), writes=["ckb"], lane="ckb")
        A("pool", lambda e: e.memset(vaug_c[:], 1.0), writes=[f"vaug_c{s4}" for s4 in range(4)])
        for s4 in range(4):
            S.dma("pool", vaug_c[:, s4, :, 0:64], cv_d[s4].rearrange("t (g d) -> t g d", g=2), reads=[f"vaug_c{s4}"], writes=[f"vaug_c{s4}"], lane=f"vaug_c{s4}")

        def setup_part2():
            S.tag = "setup2"
            for g in range(8):
                A("pe", lambda e, g=g: e.transpose(pTr[:, g, :], wsrc[:, g, :], ident[:]), reads=["cat_a", "cat_c", "ident"], writes=["pTr"])
            A("dve", lambda e: e.tensor_tensor(WT[:], pTr[:], maskc[:].unsqueeze(1).to_broadcast([128, 8, 128]), op=ALU.mult),
              reads=["pTr", "maskc"], writes=["WT"])
            w_out_v = w_out_d.rearrange("(k p) n -> p k n", p=128)
            for k in range(8):
                S.dma("pool", w_out[:, k, :], w_out_v[:, k, :], writes=[f"w_out{k}"], lane=f"w_out{k}")

        def setup_part2b(ks=range(8)):
            S.tag = "setup2b"
            for k in ks:
                A("act", lambda e, k=k: e.activation(w_out[:, k, :], w_out[:, k, :], AF.Copy, scale=gcat[:, k:k + 1]),
                  reads=[f"w_out{k}", "gcat"], writes=[f"w_out{k}"])

        def setup_part2c():
            S.tag = "setup2c"
            for g in range(8):
                A("pe", lambda e, g=g: e.transpose(pTr[:, g, :], wsrc2[:, g, :], ident[:]), reads=["wsrc2a", "wsrc2b", "ident"], writes=["pTr"])
            A("dve", lambda e: e.tensor_copy(WTs[:], pTr[:]), reads=["pTr"], writes=["WTs"])
            for s4 in range(4):
                A("pe", lambda e, s4=s4: e.transpose(pTr[:, s4, :], ckb[:, s4, :], ident[:]), reads=["ckb", "ident"], writes=["pTr"])
            A("dve", lambda e: e.tensor_copy(kTc[:], pTr[:, 0:4, :]), reads=["pTr"], writes=["kTc"])

        w_gu_v = w_gu_d.rearrange("(k p) n -> p k n", p=128)

        def load_group(G, parts=(0, 1, 2)):
            g = G % NG
            s = G % NSLOT
            c0 = g * GC * 128
            key = f"ring{s}"
            if 0 in parts:
                S.dma("pool", ring_g[s][:], w_gu_v[:, :, c0:c0 + GC * 128], writes=[key], lane=key)
            if 1 in parts:
                S.dma("pool", ring_u[s][:], w_gu_v[:, :, D_FF + c0:D_FF + c0 + GC * 128], writes=[key], lane=key)
            if 2 in parts:
                S.dma("pool", ring_d[s][:], w_dn_d[c0:c0 + GC * 128, :].rearrange("(c p) n -> p c n", p=128), writes=[key], lane=key)

        def tinfo(ti):
            kind, j = TILES[ti]
            is_s = kind == "s"
            return dict(kind=kind, j=j, is_s=is_s, ridx=16 if is_s else j, out_tile=is_s or j == 15,
                        lt=ti % TPS, xs=ti % 2, kv=2 if is_s else (j % 2))

        def stage_A1a(ti):
            S.tag = "A1a(%d)" % ti
            for k in range(8):
                A("pe", lambda e, k=k: e.transpose(pTr[:, k, :], xb1[:, k * 128:(k + 1) * 128], ident[:]), reads=["xb", "ident"], writes=["pTr"])
            A("dve", lambda e: e.tensor_copy(xT[:], pTr[:]), reads=["pTr"], writes=["xT"])
            if ti + 1 < NT:
                load_xb(ti + 1)

        def stage_A1(ti):
            S.tag = "A1(%d)" % ti
            I = tinfo(ti)
            j, is_s, kv = I["j"], I["is_s"], I["kv"]
            xs_ = I["xs"]
            for (b, c0, w) in ((2, 0, 512), (3, 512, 256)):
                for k in range(8):
                    A("pe", lambda e, b=b, c0=c0, w=w, k=k: e.matmul(bank[b][:, 0:w], xT[:, k, :], w_in[:, k, c0:c0 + w], start=(k == 0), stop=(k == 7)),
                      reads=["xT", f"w_inA{k}"], writes=[f"bank{b}"])
            q_nat = bank[2][:].rearrange("p (g c d) -> p c g d", g=2, c=4)
            A("act", lambda e: e.activation(qkb[:, 0:512].rearrange("p (c g d) -> p c g d", c=4, g=2), q_nat, AF.Copy), reads=["bank2"], writes=["qkb"])
            A("act", lambda e: e.activation(xr[:, 0:8, :].rearrange("p (c g) r -> p c g r", c=4), q_nat[:, :, :, 0:16], AF.Copy), reads=["bank2"], writes=["xr"])
            k_nat = bank[3][:, 0:128].rearrange("p (g d) -> p g d", g=2)
            A("act", lambda e: e.activation(qkb[:, 512:640], bank[3][:, 0:128], AF.Copy), reads=["bank3", "qkb"], writes=["qkb"])
            A("act", lambda e: e.activation(xr[:, 8:10, :], k_nat[:, :, 0:16], AF.Copy), reads=["bank3", "xr"], writes=["xr"])
            A("act", lambda e: e.activation(vaug[kv][:, :, 0:64], bank[3][:, 128:256].rearrange("p (g d) -> p g d", g=2), AF.Copy),
              reads=["bank3"], writes=[f"vaug{kv}"])
            if I["out_tile"]:
                A("act", lambda e: e.activation(kout[:], bank[3][:, 0:128], AF.Copy), reads=["bank3"], writes=["kout"])
                A("act", lambda e: e.activation(vf[:], bank[3][:, 128:256], AF.Copy), reads=["bank3"], writes=["vf"])
            qb3 = qkb[:].rearrange("p (h d) -> p h d", h=10)
            ridx = I["ridx"]
            cosb = rope[:, ridx, 0:8].unsqueeze(1).to_broadcast([128, 10, 8])
            sinb = rope[:, ridx, 8:16].unsqueeze(1).to_broadcast([128, 10, 8])
            A("dve", lambda e: e.tensor_tensor(rt[0][:], xr[:, :, 0:8], cosb, op=ALU.mult), reads=["xr", "rope"], writes=["rt0"])
            A("dve", lambda e: e.tensor_tensor(rt[1][:], xr[:, :, 8:16], sinb, op=ALU.mult), reads=["xr", "rope"], writes=["rt1"])
            A("dve", lambda e: e.tensor_tensor(rt[2][:], xr[:, :, 8:16], cosb, op=ALU.mult), reads=["xr", "rope"], writes=["rt2"])
            A("dve", lambda e: e.tensor_tensor(rt[3][:], xr[:, :, 0:8], sinb, op=ALU.mult), reads=["xr", "rope"], writes=["rt3"])
            A("dve", lambda e: e.tensor_tensor(qb3[:, :, 0:8], rt[0][:], rt[1][:], op=ALU.subtract), reads=["rt0", "rt1", "qkb"], writes=["qkb"])
            A("dve", lambda e: e.tensor_tensor(qb3[:, :, 8:16], rt[2][:], rt[3][:], op=ALU.add), reads=["rt2", "rt3", "qkb"], writes=["qkb"])
            if I["out_tile"]:
                ko3 = kout[:].rearrange("p (h d) -> p h d", h=2)
                A("pool", lambda e: e.tensor_tensor(ko3[:, :, 0:8], rt[0][:, 8:10, :], rt[1][:, 8:10, :], op=ALU.subtract), reads=["rt0", "rt1", "kout"], writes=["kout"])
                A("pool", lambda e: e.tensor_tensor(ko3[:, :, 8:16], rt[2][:, 8:10, :], rt[3][:, 8:10, :], op=ALU.add), reads=["rt2", "rt3", "kout"], writes=["kout"])
                if is_s:
                    S.dma("sp", ks_d[j * 128:(j + 1) * 128, :], kout[:], reads=["kout"], lane="o_k")
                    S.dma("sp", vs_d[j * 128:(j + 1) * 128, :], vf[:], reads=["vf"], lane="o_v")
                else:
                    S.dma("sp", kp_d, kout[:], reads=["kout"], lane="o_k")
                    S.dma("sp", vp_d, vf[:], reads=["vf"], lane="o_v")

        def stage_A2(ti):
            S.tag = "A2(%d)" % ti
            I = tinfo(ti)
            j, is_s = I["j"], I["is_s"]
            for (b, c0) in ((4, 768), (5, 1280)):
                for k in range(8):
                    A("pe", lambda e, b=b, c0=c0, k=k: e.matmul(bank[b][:], xT[:, k, :], w_in[:, k, c0:c0 + 512], start=(k == 0), stop=(k == 7)),
                      reads=["xT", f"w_inB{k}"], writes=[f"bank{b}"])

        def stage_A2y(ti):
            S.tag = "A2y(%d)" % ti
            I = tinfo(ti)
            j, is_s = I["j"], I["is_s"]
            A("act", lambda e: e.activation(ug[:], bank[4][:], AF.Gelu_apprx_tanh), reads=["bank4"], writes=["ug"])
            A("act", lambda e: e.activation(vg[:], bank[5][:], AF.Gelu_apprx_tanh), reads=["bank5"], writes=["vg"])

        def stage_A2v(ti):
            S.tag = "A2v(%d)" % ti
            I = tinfo(ti)
            j, is_s = I["j"], I["is_s"]
            A("dve", lambda e: e.bn_stats(stt_v[:], vg[:]), reads=["vg"], writes=["stt_v"])
            A("dve", lambda e: e.bn_aggr(mv_v[:], stt_v[:]), reads=["stt_v"], writes=["mv_v"])
            small_rstd(mv_v[:, 1:2], sm["v"], ["mv_v"], "sm_v")
            A("dve", lambda e: e.scalar_tensor_tensor(vm[:], vg[:], mv_v[:, 0:1], gv[:], op0=ALU.subtract, op1=ALU.mult), reads=["vg", "mv_v", "gv"], writes=["vm"])
            if is_s:
                A("dve", lambda e: e.scalar_tensor_tensor(vm[:], vm[:], sm["v"][:, 0:1], bv[:], op0=ALU.mult, op1=ALU.add), reads=["vm", "sm_v", "bv"], writes=["vm"])
                A("pool", lambda e: e.tensor_copy(vmb[:], vm[:]), reads=["vm"], writes=["vmb"])
                S.dma("sp", ms_d[j * 128:(j + 1) * 128, :], vm[:], reads=["vm"], lane="o_m")
            else:
                A("dve", lambda e: e.scalar_tensor_tensor(vmb[:], vm[:], sm["v"][:, 0:1], bv[:], op0=ALU.mult, op1=ALU.add), reads=["vm", "sm_v", "bv"], writes=["vmb"])

        def stage_E(ti):
            S.tag = "E(%d)" % ti
            lt = tinfo(ti)["lt"]
            A("act", lambda e: e.activation(hb[:], acc[:, lt, :], AF.Copy), reads=[f"acc{lt}_0", f"acc{lt}_1"], writes=["hb"])
            for k in range(8):
                A("pe", lambda e, k=k: e.transpose(pTr[:, k, :], hb[:, k * 128:(k + 1) * 128], ident[:]), reads=["hb", "ident"], writes=["pTr"])
            A("act", lambda e: e.activation(hT[:, :, lt * 128:(lt + 1) * 128], pTr[:], AF.Copy), reads=["pTr"], writes=[f"hT{lt // 3}"])

        def key_tiles(I):
            j = I["j"]
            if I["is_s"]:
                sa, sb_ = 2 * j, 2 * j + 1
                return [
                    (lambda g: kTc[g * 64:(g + 1) * 64, sa, :], "kTc", (0, 64), lambda g: vaug_c[:, sa, g, :], f"vaug_c{sa}", [((0, 128), (64, 128))]),
                    (lambda g: kTc[g * 64:(g + 1) * 64, sb_, :], "kTc", (64, 128), lambda g: vaug_c[:, sb_, g, :], f"vaug_c{sb_}", [((0, 128), (0, 64))]),
                    (lambda g: kT[2][g * 64:(g + 1) * 64, :], "kT2", (0, 128), lambda g: vaug[2][:, g, :], "vaug2",
                     [((0, 64), (64, 128)), ((64, 128), (0, 64))]),
                ]
            cur, prv = j % 2, (j - 1) % 2
            kts = []
            if j > 0:
                kts.append((lambda g: kT[prv][g * 64:(g + 1) * 64, :], f"kT{prv}", (0, 128), lambda g: vaug[prv][:, g, :], f"vaug{prv}", [((0, 64), (64, 128))]))
            kts.append((lambda g: kT[cur][g * 64:(g + 1) * 64, :], f"kT{cur}", (0, 128), lambda g: vaug[cur][:, g, :], f"vaug{cur}", [((64, 128), (0, 64))]))
            return kts

        def stage_B(ti):
            S.tag = "B(%d)" % ti
            I = tinfo(ti)
            kv = I["kv"]
            for c in range(5):
                A("pe", lambda e, c=c: e.transpose(pTr[:, c, :], qkb[:, c * 128:(c + 1) * 128], ident[:]), reads=["qkb", "ident"], writes=["pTr"])

        def stage_Bc(ti):
            S.tag = "Bc(%d)" % ti
            kv = tinfo(ti)["kv"]
            A("act", lambda e: e.activation(qT[:], pTr[:, 0:4, :], AF.Copy), reads=["pTr"], writes=["qT"])
            A("act", lambda e: e.activation(kT[kv][:], pTr[:, 4, :], AF.Copy), reads=["pTr"], writes=[f"kT{kv}"])
            A("act", lambda e: e.activation(dmy[:], c_mh[:], AF.Exp), reads=["c_mh", "dmy"], writes=["dmy"])

        def stage_B2(ti):
            S.tag = "B2(%d)" % ti
            I = tinfo(ti)
            bi = 0
            for g in range(2):
                for kt, (kget, kkey, (q0, q1), vget, vkey, zb) in enumerate(key_tiles(I)):
                    b = bi % 4
                    bi += 1
                    nq = q1 - q0
                    pS = bank[b][:, 0:4 * nq].rearrange("p (c q) -> p c q", c=4)
                    A("pe", lambda e, pS=pS, kget=kget, g=g, q0=q0, q1=q1: e.matmul(pS, kget(g), qT[g * 64:(g + 1) * 64, :, q0:q1], start=True, stop=True),
                      reads=[kkey, "qT"], writes=[f"bank{b}"])
                    pt = PT[g][kt]
                    ptk = f"PT{g}{kt}"
                    A("act", lambda e, pt=pt, pS=pS, q0=q0, q1=q1: e.activation(pt[:, :, q0:q1], pS, AF.Exp, scale=0.125), reads=[f"bank{b}"], writes=[ptk])
                    for ((p0, p1), (z0, z1)) in zb:
                        A("pool", lambda e, pt=pt, p0=p0, p1=p1, z0=z0, z1=z1: e.memset(pt[p0:p1, :, z0:z1], 0.0), reads=[ptk], writes=[ptk])

            A("act", lambda e: e.activation(dmy[:], c_mh[:], AF.Gelu_apprx_tanh), reads=["c_mh"], writes=["dmy"])

        def stage_C(ti):
            S.tag = "C(%d)" % ti
            I = tinfo(ti)
            is_s = I["is_s"]
            kts = key_tiles(I)
            nkt = len(kts)
            Wt = WTs if is_s else WT
            bt = bTs if is_s else bT
            pG = bank[6][:].rearrange("p (g d) -> p g d", g=8)
            for g in range(8):
                A("pe", lambda e, g=g, Wt=Wt: e.matmul(pG[:, g, :], Wt[:, g, :], vmb[:, g * 64:(g + 1) * 64], start=True, stop=True),
                  reads=["WT", "WTs", "vmb"], writes=["bank6"])
            co3 = co[:].rearrange("p (g d) -> p g d", g=8)
            A("dve", lambda e, bt=bt: e.tensor_tensor(co3, pG, bt[:].unsqueeze(2).to_broadcast([128, 8, 64]), op=ALU.add), reads=["bank6", "bT", "bTs"], writes=["co"])
            A("pool", lambda e: e.tensor_tensor(cat[:, 512:1024], co[:], ug[:], op=ALU.mult), reads=["co", "ug"], writes=["cat_c"])
            for g in range(2):
                pO = bank[g][:, 0:260].rearrange("p (c d) -> p c d", c=4)
                for c in range(4):
                    for kt, (kget, kkey, (q0, q1), vget, vkey, zb) in enumerate(kts):
                        A("pe", lambda e, pO=pO, g=g, c=c, kt=kt, vget=vget: e.matmul(pO[:, c, :], PT[g][kt][:, c, :], vget(g), start=(kt == 0), stop=(kt == nkt - 1)),
                          reads=[f"PT{g}{kt}", vkey], writes=[f"bank{g}"])
            for g in range(2):
                pO = bank[g][:, 0:260].rearrange("p (c d) -> p c d", c=4)
                A("dve", lambda e, pO=pO, g=g: e.tensor_tensor(den[:, g * 4:(g + 1) * 4], pO[:, :, 64], esink[:, g * 4:(g + 1) * 4], op=ALU.add),
                  reads=[f"bank{g}", "esink"], writes=[f"den{g}"])
            A("dve", lambda e: e.reciprocal(rden[:], den[:]), reads=["den0", "den1"], writes=["rden"])
            for g in range(2):
                pO = bank[g][:, 0:260].rearrange("p (c d) -> p c d", c=4)
                ca3 = cat[:, g * 256:(g + 1) * 256].rearrange("p (c d) -> p c d", c=4)
                A("dve", lambda e, pO=pO, ca3=ca3, g=g: e.tensor_tensor(ca3, pO[:, :, 0:64], rden[:, g * 4:(g + 1) * 4].unsqueeze(2).to_broadcast([128, 4, 64]), op=ALU.mult),
                  reads=[f"bank{g}", "rden"], writes=["cat_a"])

        def stage_C2(ti):
            S.tag = "C2(%d)" % ti
            A("act", lambda e: e.activation(sqj, cat[:, 0:512], AF.Square, scale=float(ALPHA * 512 ** -0.5), accum_out=sm["a1"][:, 0:1]), reads=["cat_a"], writes=["vg", "sm_a1"])
            small_rstd(sm["a1"][:, 0:1], sm["a2"], ["sm_a1"], "sm_a2", c_eps_r, "c_eps_r")
            A("act", lambda e: e.activation(co[:], cat[:, 512:1024], AF.Square, scale=float(ALPHA * 512 ** -0.5), accum_out=sm["c1"][:, 0:1]), reads=["cat_c"], writes=["co", "sm_c1"])
            small_rstd(sm["c1"][:, 0:1], sm["c2"], ["sm_c1"], "sm_c2", c_eps_r, "c_eps_r")

        def stage_D(ti):
            S.tag = "D(%d)" % ti
            I = tinfo(ti)
            lt, xs = I["lt"], I["xs"]
            for k in range(8):
                A("pe", lambda e, k=k: e.transpose(pTr2[:, k, :], cat[:, k * 128:(k + 1) * 128], ident[:]), reads=["cat_a", "cat_c", "ident"], writes=["bank6"])
            A("dve", lambda e: e.tensor_copy(catT[:], pTr2), reads=["bank6"], writes=["catT"])

        def stage_D1b(ti):
            S.tag = "D1b(%d)" % ti
            I = tinfo(ti)
            lt, xs = I["lt"], I["xs"]
            for (kb, b0) in ((0, 4), (4, 2)):
                for n in range(2):
                    for k in range(kb, kb + 4):
                        A("pe", lambda e, n=n, k=k, kb=kb, b0=b0: e.matmul(bank[b0 + n][:], catT[:, k, :], w_out[:, k, n * 512:(n + 1) * 512], start=(k == kb), stop=(k == kb + 3)),
                          reads=["catT", f"w_out{k}"], writes=[f"bank{b0 + n}"])
            for n in range(2):
                cs = slice(n * 512, (n + 1) * 512)
                A("dve", lambda e, n=n, cs=cs: e.scalar_tensor_tensor(pre[:, cs], bank[4 + n][:], sm["a2"][:, 0:1], xf[xs][:, cs], op0=ALU.mult, op1=ALU.add),
                  reads=[f"xf{xs}", f"bank{4 + n}", "sm_a2"], writes=[f"pre{n}"])
                A("dve", lambda e, n=n, cs=cs: e.scalar_tensor_tensor(pre[:, cs], bank[2 + n][:], sm["c2"][:, 0:1], pre[:, cs], op0=ALU.mult, op1=ALU.add),
                  reads=[f"pre{n}", f"bank{2 + n}", "sm_c2"], writes=[f"pre{n}"])
                A("dve", lambda e, n=n, cs=cs: e.bn_stats(stt[:, 6 * n:6 * n + 6], pre[:, cs]), reads=[f"pre{n}"], writes=["stt"])
            if ti + 2 < NT:
                load_xf(ti + 2)
            A("dve", lambda e: e.bn_aggr(mv[:], stt[:]), reads=["stt"], writes=["mv"])
            small_rstd(mv[:, 1:2], sm["l1"], ["mv"], "sm_l1", c_eps_l, "c_eps_l")

        def stage_D2(ti):
            S.tag = "D2(%d)" % ti
            lt = tinfo(ti)["lt"]
            A("dve", lambda e: e.scalar_tensor_tensor(hf[:], pre[:], mv[:, 0:1], g1[:], op0=ALU.subtract, op1=ALU.mult), reads=["pre0", "pre1", "mv", "g1"], writes=["hf"])
            A("dve", lambda e: e.scalar_tensor_tensor(acc[:, lt, :], hf[:], sm["l1"][:, 0:1], b1[:], op0=ALU.mult, op1=ALU.add), reads=["hf", "sm_l1", "b1"], writes=[f"acc{lt}_0", f"acc{lt}_1"])

        def phase_a(sg):
            t0 = sg * TPS
            import os
            KSTOP = int(os.environ.get("KSTOP", "-1"))
            if sg == 0:
                stage_A1a(t0)
            stage_A1(t0)
            stage_A2(t0)
            if sg == 0:
                setup_part2()
            for it in range(TPS + 2):
                if KSTOP >= 0 and it >= KSTOP:
                    break
                t = t0 + it
                nxt = it + 1 < TPS
                if it < TPS:
                    stage_B(t)
                    stage_A2y(t)
                    stage_Bc(t)
                if 1 <= it <= TPS:
                    stage_D(t - 1)
                if nxt:
                    stage_A1a(t + 1)
                if it < TPS:
                    stage_A2v(t)
                    stage_B2(t)
                    if sg == 0 and it == 0:
                        setup_part2c()
                        setup_part2b(range(0, 4))
                if 1 <= it <= TPS:
                    stage_D1b(t - 1)
                if it < TPS:
                    stage_C(t)
                if 2 <= it:
                    if it == TPS + 1:
                        DEFER_E.append(t - 2)
                    else:
                        stage_E(t - 2)
                if nxt:
                    stage_A1(t + 1)
                if sg == 0 and it == 0:
                    setup_part2b(range(4, 8))
                if 1 <= it <= TPS:
                    stage_D2(t - 1)
                if it < TPS:
                    stage_C2(t)
                if nxt:
                    stage_A2(t + 1)
                if sg > 0 and it < 3:
                    for Gd in LATE:
                        load_group(Gd, parts=(it,))
                if sg == 0 and 2 <= it <= 7 and KSTOP < 0:
                    load_group((it - 2) // 3, parts=((it - 2) % 3,))

        OUTB = [bank[4][:], bank[5][:], bank[6][:], pTr[:].rearrange("p a b -> p (a b)").bitcast(F32)]
        OUTK = ["bank4", "bank5", "bank6", "pTr"]

        def phase_b(sg):
            it = 0
            passes = [[g] for g in range(NG - 2)] + [[NG - 2, NG - 1]]
            pend = []
            ln_pend = []
            late_loads = []

            def flush(parts):
                for fn in list(parts):
                    fn()

            def gate_up(t, slot, rkey, cc, ab):
                bg, bu = bank[2 * cc], bank[2 * cc + 1]
                for k in range(8):
                    A("pe", lambda e, k=k: e.matmul(bg[:, 0:TG], ring_g[slot][:, k, cc * 128:(cc + 1) * 128], hT[:, k, t * TG:(t + 1) * TG], start=(k == 0), stop=(k == 7)),
                      reads=[rkey, f"hT{t}"], writes=[f"bank{2 * cc}"])
                for k in range(8):
                    A("pe", lambda e, k=k: e.matmul(bu[:, 0:TG], ring_u[slot][:, k, cc * 128:(cc + 1) * 128], hT[:, k, t * TG:(t + 1) * TG], start=(k == 0), stop=(k == 7)),
                      reads=[rkey, f"hT{t}"], writes=[f"bank{2 * cc + 1}"])
                A("act", lambda e: e.activation(sgt[cc], bg[:, 0:TG], AF.Silu), reads=[f"bank{2 * cc}"], writes=[SGK[cc]])
                A("dve", lambda e: e.tensor_tensor(actT[ab][:, cc, :], sgt[cc], bu[:, 0:TG], op=ALU.mult),
                  reads=[SGK[cc], f"bank{2 * cc + 1}"], writes=ATK[ab])

            def down(t, i, chunks, first, last):
                lt = t * (TG // 128) + i
                for n in range(2):
                    oi = (n + 2 * i) % 4
                    obk = OUTK[oi]
                    obt = OUTB[oi]
                    for ci, (ab, cc, slot, rkey) in enumerate(chunks):
                        A("pe", lambda e, obt=obt, cc=cc, n=n, ab=ab, slot=slot, ci=ci: e.matmul(obt, actT[ab][:, cc, i * 128:(i + 1) * 128], ring_d[slot][:, cc, n * 512:(n + 1) * 512], start=(ci == 0), stop=(ci == len(chunks) - 1)),
                          reads=ATK[ab] + [rkey], writes=[obk])
                    accs = acc[:, lt, n * 512:(n + 1) * 512]
                    if first:
                        A("dve", lambda e, obt=obt, accs=accs: e.scalar_tensor_tensor(accs, accs, ALPHA, obt, op0=ALU.mult, op1=ALU.add),
                          reads=[obk, f"acc{lt}_{n}"], writes=[f"acc{lt}_{n}"])
                    else:
                        A("dve", lambda e, obt=obt, accs=accs: e.tensor_tensor(accs, obt, accs, op=ALU.add),
                          reads=[obk, f"acc{lt}_{n}"], writes=[f"acc{lt}_{n}"])
                if last:
                    ti = sg * TPS + lt
                    ys = ti % 2
                    p = str(lt % 2)
                    mean_t, ex2_t, sq_t, rs_t = sm["m" + p], sm["e" + p], sm["q" + p], sm["r" + p]
                    junk = hb[:]
                    A("act", lambda e: e.activation(junk, acc[:, lt, :], AF.Identity, scale=1.0 / 1024, accum_out=mean_t[:, 0:1]),
                      reads=[f"acc{lt}_0", f"acc{lt}_1"], writes=["hb", "sm_m" + p])
                    A("act", lambda e: e.activation(junk, acc[:, lt, :], AF.Square, scale=1.0 / 32, accum_out=ex2_t[:, 0:1]),
                      reads=[f"acc{lt}_0", f"acc{lt}_1"], writes=["hb", "sm_e" + p])
                    A("pool", lambda e: e.tensor_tensor(sq_t[:], mean_t[:], mean_t[:], op=ALU.mult), reads=["sm_m" + p], writes=["sm_q" + p])
                    A("pool", lambda e: e.tensor_tensor(ex2_t[:], ex2_t[:], sq_t[:], op=ALU.subtract), reads=["sm_e" + p, "sm_q" + p], writes=["sm_e" + p])
                    small_rstd(ex2_t[:, 0:1], rs_t, ["sm_e" + p], "sm_r" + p)
                    flush(ln_pend)
                    ln_pend.clear()

                    def tail(lt=lt, ys=ys, ti=ti, p=p, mean_t=mean_t, rs_t=rs_t):
                        A("dve", lambda e: e.scalar_tensor_tensor(yo[ys][:], acc[:, lt, :], mean_t[:, 0:1], g1[:], op0=ALU.subtract, op1=ALU.mult),
                          reads=[f"acc{lt}_0", f"acc{lt}_1", "sm_m" + p, "g1"], writes=YOK[ys])
                        A("dve", lambda e: e.scalar_tensor_tensor(yo[ys][:], yo[ys][:], rs_t[:, 0:1], b1[:], op0=ALU.mult, op1=ALU.add),
                          reads=YOK[ys] + ["sm_r" + p, "b1"], writes=YOK[ys])
                        S.dma("sp", y_d[ti * 128:(ti + 1) * 128, :], yo[ys][:], reads=YOK[ys], lane=f"o_y{ys}")
                    ln_pend.append(tail)

            for gl in passes:
                first = gl[0] == 0
                last = gl[-1] == NG - 1
                merged = len(gl) > 1
                loads = []
                deferred = []
                for gi, g in enumerate(gl):
                    G = sg * NG + g
                    if G + 2 < NSG * NG:
                        (loads if gi == 0 else deferred).append(G + 2)
                for t in range(NTG):
                    chunks = []
                    specs = []
                    for gi, g in enumerate(gl):
                        G = sg * NG + g
                        slot = G % NSLOT
                        ab = (2 * ((t + 1) % 2) + gi) if merged else it % 2
                        for cc in range(GC):
                            specs.append((slot, f"ring{slot}", cc, ab))
                            chunks.append((ab, cc, slot, f"ring{slot}"))
                    it += 1
                    for si, (slot, rkey, cc, ab) in enumerate(specs):
                        gate_up(t, slot, rkey, cc, ab)
                        if DEFER_E:
                            stage_E(DEFER_E.pop())
                        if pend:
                            take = 1 if si + 1 < len(specs) else len(pend)
                            flush(pend[:take])
                            pend = pend[take:]
                    if t == 0:
                        for Gl in loads:
                            load_group(Gl)
                    parts = [lambda t=t, i=i, chunks=chunks, first=first, last=last: down(t, i, chunks, first, last) for i in range(TG // 128)]
                    pend = parts
                late_loads.extend(deferred)
            flush(pend)
            pend = []
            flush(ln_pend)
            ln_pend.clear()
            LATE.extend(late_loads)

        import os
        KSTOP = int(os.environ.get("KSTOP", "-1"))
        LATE = []
        DEFER_E = []
        for sg in range(NSG):
            load_ln(1)
            phase_a(sg)
            if KSTOP >= 0:
                break
            load_ln(2)
            if sg + 1 < NSG:
                stage_A1a((sg + 1) * TPS)
            phase_b(sg)

        S.emit(st, final_wait_lanes=["o_k", "o_v", "o_m", "o_y0", "o_y1"])
    return nc


_NC_CACHE = {}


def _consts():
    half = 8
    inv = np.power(np.float32(500000.0), -np.arange(half, dtype=np.float32) * np.float32(2.0 / 16)).astype(np.float32)
    rope = np.zeros((128, 17, 16), np.float32)
    p = np.arange(128)
    for t in range(17):
        pos = (128 * t + p) if t < 16 else (1024 + (p % 64))
        ang = pos.astype(np.float32)[:, None] * inv[None, :]
        rope[:, t, 0:8] = np.cos(ang)
        rope[:, t, 8:16] = np.sin(ang)
    i = np.arange(128)
    mask = ((i[:, None] // 64) <= (i[None, :] // 64)).astype(np.float32)
    return np.eye(128, dtype=np.float32), rope.reshape(128, 17 * 16), mask


def kernel(x_prompt, x_sample, cache_win_k, cache_win_v, w_in, ln_v_g, ln_v_b, attn_sinks, w_spatial, b_spatial,
           norm_attn_g, norm_cmlp_g, w_out, ln1_g, ln1_b, w_gate_up, w_down, ln2_g, ln2_b):
    f = lambda a: np.ascontiguousarray(np.asarray(a, dtype=np.float32))
    x_prompt, x_sample = f(x_prompt), f(x_sample)
    ck, cv = f(cache_win_k)[0], f(cache_win_v)[0]
    if "nc" not in _NC_CACHE:
        _NC_CACHE["nc"] = build_program()
    nc = _NC_CACHE["nc"]
    ident, rope, mask = _consts()
    shared = {
        "w_in": f(w_in)[0], "w_out": f(w_out)[0], "w_gu": f(w_gate_up)[0], "w_dn": f(w_down)[0],
        "ln_v_g": f(ln_v_g), "ln_v_b": f(ln_v_b), "sinks": f(attn_sinks), "w_sp": f(w_spatial)[0], "b_sp": f(b_spatial)[0],
        "ng_a": f(norm_attn_g), "ng_c": f(norm_cmlp_g), "ln1_g": f(ln1_g), "ln1_b": f(ln1_b), "ln2_g": f(ln2_g), "ln2_b": f(ln2_b),
        "c_ident": ident, "c_rope": rope, "c_mask": mask,
    }
    in_maps = []
    for c in range(N_CORES):
        xs = x_sample[4 * c:4 * c + 4].reshape(256, D)
        xp = x_prompt[c]
        xc = np.concatenate([xp[0:1024], xs[0:128], xp[1024:2048], xs[128:256]], axis=0)
        m = dict(shared)
        m["x"] = np.ascontiguousarray(xc)
        m["cache_k"] = np.ascontiguousarray(ck[4 * c:4 * c + 4].reshape(4, 128, 128))
        m["cache_v"] = np.ascontiguousarray(cv[4 * c:4 * c + 4].reshape(4, 128, 128))
        in_maps.append(m)
    res = run_bass_kernel_spmd(nc, in_maps, core_ids=list(range(N_CORES)))
    R = res.results
    y_p = np.stack([np.concatenate([R[c]["y"][0:1024], R[c]["y"][1152:2176]], axis=0) for c in range(N_CORES)])
    y_s = np.concatenate([np.concatenate([R[c]["y"][1024:1152], R[c]["y"][2176:2304]], axis=0).reshape(4, 64, D) for c in range(N_CORES)])
    kp = np.stack([R[c]["kp"].reshape(128, 2, 64) for c in range(N_CORES)])[None]
    vp = np.stack([R[c]["vp"].reshape(128, 2, 64) for c in range(N_CORES)])[None]
    ks = np.concatenate([R[c]["ks"].reshape(4, 64, 2, 64) for c in range(N_CORES)])[None]
    vs = np.concatenate([R[c]["vs"].reshape(4, 64, 2, 64) for c in range(N_CORES)])[None]
    ms = np.concatenate([R[c]["ms"].reshape(4, 64, 8, 64) for c in range(N_CORES)])[None]
    return (y_p.astype(np.float32), y_s.astype(np.float32), kp.astype(np.float32), vp.astype(np.float32),
            ks.astype(np.float32), vs.astype(np.float32), ms.astype(np.float32))
```
